# Optimizing a Trainium2 kernel written in Bass

```python
import math
import jax, jax.numpy as jnp
from jax import lax
import numpy as np

D_MODEL = 2048
BATCH = 2
SEQ = 8192
DEPTH = 1

GRID_W = 64
CTX_LEN = 256

DA_HEADS = 8
DA_HEAD_DIM = 64
DA_V = 2 * DA_HEAD_DIM
MLA_HEADS = 8
MLA_Q_RANK = 512
MLA_KV_RANK = 256
MLA_NOPE = 128
MLA_ROPE = 64
MLA_V = 128
MLA_SCALE = (MLA_NOPE + MLA_ROPE) ** -0.5
ROPE_DIM = 64
ROPE_THETA = 10000.0
FFN_HIDDEN = -(-8 * D_MODEL // (3 * 256)) * 256

Q_BLOCK = 128
EPS = 1e-6

DA_Q_W = DA_HEADS * 2 * DA_HEAD_DIM
DA_K_W = DA_HEADS * 2 * DA_HEAD_DIM
DA_V_W = DA_HEADS * DA_V
_IN_SIZES = (DA_Q_W, DA_K_W, DA_V_W, MLA_Q_RANK, MLA_KV_RANK, MLA_ROPE, D_MODEL, D_MODEL)
_IN_SPLITS = tuple(sum(_IN_SIZES[:i + 1]) for i in range(len(_IN_SIZES) - 1))
IN_W = sum(_IN_SIZES)

kernel_name = "hybrid_diffattn_mla_gated_dit_block"


def _rmsnorm(x, g):
    xf = x.astype(jnp.float32)
    y = xf * lax.rsqrt(jnp.mean(xf * xf, axis=-1, keepdims=True) + EPS)
    return (y * g.astype(jnp.float32)).astype(x.dtype)


def _modulate(h, shift, scale):
    return h * (1 + scale) + shift


def _swiglu(h, w_gate, w_up, w_down):
    return (jax.nn.silu(h @ w_gate) * (h @ w_up)) @ w_down


def _rope_tables(n):
    t = jnp.arange(n, dtype=jnp.int32)
    row = (t // GRID_W).astype(jnp.float32)
    col = (t % GRID_W).astype(jnp.float32)
    nf = ROPE_DIM // 4
    inv = ROPE_THETA ** (-jnp.arange(nf, dtype=jnp.float32) / nf)
    ar = row[:, None] * inv
    ac = col[:, None] * inv
    return (jnp.cos(ar), jnp.sin(ar), jnp.cos(ac), jnp.sin(ac))


def _rope_half(x, cos, sin):
    x1, x2 = jnp.split(x.astype(jnp.float32), 2, axis=-1)
    c = cos[None, :, None, :]
    s = sin[None, :, None, :]
    return jnp.concatenate([x1 * c - x2 * s, x2 * c + x1 * s], axis=-1)


def _axial_rope(x, tabs):
    cr, sr, cc, sc = tabs
    xr, xc = jnp.split(x, 2, axis=-1)
    return jnp.concatenate([_rope_half(xr, cr, sr), _rope_half(xc, cc, sc)], axis=-1).astype(x.dtype)


def _sweep_queries(q, block_fn):
    b, n, h, d = q.shape
    nb = n // Q_BLOCK
    qb = q.reshape(b, nb, Q_BLOCK, h, d).transpose(1, 0, 2, 3, 4)
    out = lax.map(block_fn, qb)
    return out.transpose(1, 0, 2, 3, 4).reshape(b, n, out.shape[3], out.shape[4])


def _diff_attend(q, k, v, lam):
    s = jnp.einsum("bqhd,bkhd->bhqk", q, k, preferred_element_type=jnp.float32) * (DA_HEAD_DIM ** -0.5)
    p = jax.nn.softmax(s, axis=-1)
    b, hh, nq, nk = p.shape
    p = p.reshape(b, hh // 2, 2, nq, nk)
    a = p[:, :, 0] - lam * p[:, :, 1]
    return jnp.einsum("bhqk,bkhd->bqhd", a.astype(v.dtype), v)


def _softmax_attend(q, k, v, scale):
    s = jnp.einsum("bqhd,bkhd->bhqk", q, k, preferred_element_type=jnp.float32) * scale
    p = jax.nn.softmax(s, axis=-1)
    return jnp.einsum("bhqk,bkhd->bqhd", p.astype(v.dtype), v)


def _project(h, w_in, mla_q_g, mla_kv_g, w_uq, w_ukv, tabs):
    b, n, _ = h.shape
    z = h @ w_in
    dq, dk, dv, cq, ckv, kr, ga, gb = jnp.split(z, _IN_SPLITS, axis=-1)
    dq = dq.reshape(b, n, 2 * DA_HEADS, DA_HEAD_DIM)
    dk = dk.reshape(b, n, 2 * DA_HEADS, DA_HEAD_DIM)
    dv = dv.reshape(b, n, DA_HEADS, DA_V)
    q = (_rmsnorm(cq, mla_q_g) @ w_uq).reshape(b, n, MLA_HEADS, MLA_NOPE + MLA_ROPE)
    kv = (_rmsnorm(ckv, mla_kv_g) @ w_ukv).reshape(b, n, MLA_HEADS, MLA_NOPE + MLA_V)
    q_nope, q_rope = jnp.split(q, [MLA_NOPE], axis=-1)
    k_nope, mv = jnp.split(kv, [MLA_NOPE], axis=-1)
    k_rope = kr[:, :, None, :]
    if tabs is not None:
        dq = _axial_rope(dq, tabs)
        dk = _axial_rope(dk, tabs)
        q_rope = _axial_rope(q_rope, tabs)
        k_rope = _axial_rope(k_rope, tabs)
    mq = jnp.concatenate([q_nope, q_rope], axis=-1)
    mk = jnp.concatenate([k_nope, jnp.broadcast_to(k_rope, (b, n, MLA_HEADS, MLA_ROPE))], axis=-1)
    return dq, dk, dv, mq, mk, mv, ga, gb


def _merge(o_da, o_mla, ga, gb, da_g, lam_init, w_o_da, w_o_mla, w_out):
    b, n = o_da.shape[:2]
    o_da = _rmsnorm(o_da, da_g) * (1.0 - lam_init)
    y_a = o_da.reshape(b, n, DA_HEADS * DA_V) @ w_o_da
    y_b = o_mla.reshape(b, n, MLA_HEADS * MLA_V) @ w_o_mla
    return (jax.nn.sigmoid(ga) * y_a + jax.nn.sigmoid(gb) * y_b) @ w_out


def setup_inputs(seed: int = 0) -> dict:
    key = jax.random.key(seed)
    ks = jax.random.split(key, 22)
    f32 = jnp.float32

    def w(k, shape, fan_in):
        return jax.random.normal(k, shape, f32) * fan_in ** -0.5

    def gain(k, shape):
        return 1.0 + 0.02 * jax.random.normal(k, shape, f32)

    return {
        "x": jax.random.normal(ks[0], (BATCH, SEQ, D_MODEL), f32),
        "c": jax.random.normal(ks[1], (BATCH, D_MODEL), f32),
        "ctx": jax.random.normal(ks[2], (BATCH, CTX_LEN, D_MODEL), f32),
        "c_ctx": jax.random.normal(ks[3], (D_MODEL,), f32),
        "w_ada": w(ks[4], (DEPTH, D_MODEL, 6 * D_MODEL), D_MODEL),
        "b_ada": 0.01 * jax.random.normal(ks[5], (DEPTH, 6 * D_MODEL), f32),
        "norm1_g": gain(ks[6], (DEPTH, D_MODEL)),
        "norm2_g": gain(ks[7], (DEPTH, D_MODEL)),
        "w_in": w(ks[8], (DEPTH, D_MODEL, IN_W), D_MODEL),
        "da_lambda": 0.1 * jax.random.normal(ks[9], (DEPTH, 4, DA_HEAD_DIM), f32),
        "da_subln_g": gain(ks[10], (DEPTH, DA_V)),
        "mla_q_norm_g": gain(ks[11], (DEPTH, MLA_Q_RANK)),
        "mla_kv_norm_g": gain(ks[12], (DEPTH, MLA_KV_RANK)),
        "w_uq": w(ks[13], (DEPTH, MLA_Q_RANK, MLA_HEADS * (MLA_NOPE + MLA_ROPE)), MLA_Q_RANK),
        "w_ukv": w(ks[14], (DEPTH, MLA_KV_RANK, MLA_HEADS * (MLA_NOPE + MLA_V)), MLA_KV_RANK),
        "w_o_da": w(ks[15], (DEPTH, DA_HEADS * DA_V, D_MODEL), DA_HEADS * DA_V),
        "w_o_mla": w(ks[16], (DEPTH, MLA_HEADS * MLA_V, D_MODEL), MLA_HEADS * MLA_V),
        "w_out": w(ks[17], (DEPTH, D_MODEL, D_MODEL), D_MODEL),
        "w_ffn_gate": w(ks[18], (DEPTH, D_MODEL, FFN_HIDDEN), D_MODEL),
        "w_ffn_up": w(ks[19], (DEPTH, D_MODEL, FFN_HIDDEN), D_MODEL),
        "w_ffn_down": w(ks[20], (DEPTH, FFN_HIDDEN, D_MODEL), FFN_HIDDEN),
        "final_norm_g": gain(ks[21], (D_MODEL,)),
    }


def reference(x, c, ctx, c_ctx, w_ada, b_ada, norm1_g, norm2_g, w_in, da_lambda, da_subln_g,
              mla_q_norm_g, mla_kv_norm_g, w_uq, w_ukv, w_o_da, w_o_mla, w_out,
              w_ffn_gate, w_ffn_up, w_ffn_down, final_norm_g):
    n = x.shape[1]
    tabs = _rope_tables(n)
    for i in range(DEPTH):
        last = i == DEPTH - 1
        lam_init = 0.8 - 0.6 * math.exp(-0.3 * i)
        lp = da_lambda[i].astype(jnp.float32)
        lam = jnp.exp(jnp.sum(lp[0] * lp[1])) - jnp.exp(jnp.sum(lp[2] * lp[3])) + lam_init

        mod = (jax.nn.silu(c) @ w_ada[i] + b_ada[i])[:, None, :]
        mod_c = jax.nn.silu(c_ctx) @ w_ada[i] + b_ada[i]
        sh1, sc1, g1, sh2, sc2, g2 = jnp.split(mod, 6, axis=-1)
        csh1, csc1, cg1, csh2, csc2, cg2 = jnp.split(mod_c, 6, axis=-1)

        h_lat = _modulate(_rmsnorm(x, norm1_g[i]), sh1, sc1)
        h_ctx = _modulate(_rmsnorm(ctx, norm1_g[i]), csh1, csc1)
        dq, dk, dv, mq, mk, mv, ga, gb = _project(
            h_lat, w_in[i], mla_q_norm_g[i], mla_kv_norm_g[i], w_uq[i], w_ukv[i], tabs)
        cdq, cdk, cdv, cmq, cmk, cmv, cga, cgb = _project(
            h_ctx, w_in[i], mla_q_norm_g[i], mla_kv_norm_g[i], w_uq[i], w_ukv[i], None)

        dk_all = jnp.concatenate([cdk, dk], axis=1)
        dv_all = jnp.concatenate([cdv, dv], axis=1)
        mk_all = jnp.concatenate([cmk, mk], axis=1)
        mv_all = jnp.concatenate([cmv, mv], axis=1)
        o_da = _sweep_queries(dq, lambda qb: _diff_attend(qb, dk_all, dv_all, lam))
        o_mla = _sweep_queries(mq, lambda qb: _softmax_attend(qb, mk_all, mv_all, MLA_SCALE))
        x = x + g1 * _merge(o_da, o_mla, ga, gb, da_subln_g[i], lam_init, w_o_da[i], w_o_mla[i], w_out[i])
        x = x + g2 * _swiglu(_modulate(_rmsnorm(x, norm2_g[i]), sh2, sc2),
                             w_ffn_gate[i], w_ffn_up[i], w_ffn_down[i])

        if not last:
            co_da = _diff_attend(cdq, cdk, cdv, lam)
            co_mla = _softmax_attend(cmq, cmk, cmv, MLA_SCALE)
            ctx = ctx + cg1 * _merge(co_da, co_mla, cga, cgb, da_subln_g[i], lam_init,
                                     w_o_da[i], w_o_mla[i], w_out[i])
            ctx = ctx + cg2 * _swiglu(_modulate(_rmsnorm(ctx, norm2_g[i]), csh2, csc2),
                                      w_ffn_gate[i], w_ffn_up[i], w_ffn_down[i])
    return _rmsnorm(x, final_norm_g)
```

```python
import numpy as np
import concourse.bass as bass
import concourse.mybir as mybir
from concourse.bass_utils import run_bass_kernel_spmd

F32 = mybir.dt.float32
BF16 = mybir.dt.bfloat16
AF = mybir.ActivationFunctionType
ALU = mybir.AluOpType
AXX = mybir.AxisListType.X

D = 2048
KC = 16
NTK = 8448
NKT = 66
NQ = 2048
NQT = 16
FH = 5632
FHC = 44
EPS = 1e-6
LAM_INIT = 0.8 - 0.6 * 1.0
SB_BASE = 16512
SB_END = 229312
DA_SCALE = 64 ** -0.5
MLA_SCALE = 192 ** -0.5


class Sem:
    _all = []

    def __init__(self, nc, name):
        self.h = nc.alloc_semaphore(name=name)
        self.n = 0
        self.uid = len(Sem._all)
        Sem._all.append(self)


class Prog:
    ENG = ("pe", "act", "dve", "pool", "sp")

    def __init__(self, nc):
        self.nc = nc
        self.q = {e: [] for e in self.ENG}
        self.seen = {e: {} for e in self.ENG}
        self.nsem = 0
        self.prog = {e: Sem(nc, "prog_" + e) for e in ("pe", "act", "dve", "pool")}

    def sem(self, name, sw=False):
        if not hasattr(self, "pool_free"):
            self.pool_free = {False: [], True: []}
            self.pool_used = {False: [], True: []}
        if self.pool_free[sw]:
            s = self.pool_free[sw].pop()
        else:
            self.nsem += 1
            s = Sem(self.nc, "dsem%s_%d" % ("sw" if sw else "hw", self.nsem))
        self.pool_used[sw].append(s)
        return s

    def op(self, eng, meth, *args, waits=(), sig=False, dsem=None, dep=False, **kw):
        q = self.q[eng]
        w = []
        if dep and q:
            prev = q[-1]
            if prev[4] is None:
                s = self.prog[eng]
                s.n += 1
                prev[4] = (s, 1, s.n)
            w.append((prev[4][0], prev[4][2]))

        def add(t):
            if t is None:
                return
            if isinstance(t, list):
                for u in t:
                    add(u)
                return
            w.append(t)
        for t in waits:
            add(t)
        ww = []
        seen = self.seen[eng]
        for (s, v) in w:
            if v <= 0 or seen.get(s.uid, 0) >= v:
                continue
            seen[s.uid] = v
            ww.append((s, v))
        inc = None
        tok = None
        if dsem is not None:
            dsem.n += 16
            inc = (dsem, 16, dsem.n)
            tok = (dsem, dsem.n)
        elif sig:
            s = self.prog[eng]
            s.n += 1
            inc = (s, 1, s.n)
            tok = (s, s.n)
        q.append([meth, args, kw, ww, inc])
        return tok

    def last_tok(self, eng):
        prev = self.q[eng][-1]
        if prev[4] is None:
            s = self.prog[eng]
            s.n += 1
            prev[4] = (s, 1, s.n)
        return (prev[4][0], prev[4][2])

    def simulate(self):
        if not hasattr(self, "simcnt"):
            self.simcnt = {}
        cnt = self.simcnt
        pos = {e: 0 for e in self.ENG}
        progress = True
        while progress:
            progress = False
            for e in self.ENG:
                q = self.q[e]
                while pos[e] < len(q):
                    meth, args, kw, waits, inc = q[pos[e]]
                    ok = all(cnt.get(s.uid, 0) >= v for (s, v) in waits)
                    if not ok:
                        break
                    if inc is not None:
                        cnt[inc[0].uid] = cnt.get(inc[0].uid, 0) + inc[1]
                    pos[e] += 1
                    progress = True
        stuck = {e: pos[e] for e in self.ENG if pos[e] < len(self.q[e])}
        if stuck:
            msg = []
            for e, p in stuck.items():
                meth, args, kw, waits, inc = self.q[e][p]
                bad = [(s.h, v, cnt.get(s.uid, 0)) for (s, v) in waits if cnt.get(s.uid, 0) < v]
                msg.append("%s stuck at op %d/%d %s out=%s unsatisfied=%s" % (e, p, len(self.q[e]), meth,
                           str(kw.get("out", ""))[:80], bad))
            raise RuntimeError("DEADLOCK in sync plan:\n" + "\n".join(msg))

    def flush(self):
        nc = self.nc
        self.simulate()
        if hasattr(self, "pool_used"):
            for k in (False, True):
                self.pool_free[k].extend(self.pool_used[k])
                self.pool_used[k] = []
        import os as _os
        if _os.environ.get("KSTAT"):
            print("PHASE stats:", {e: (len(self.q[e]), sum(len(o[3]) for o in self.q[e])) for e in self.ENG},
                  "nsem", len(Sem._all), "sem ids", [str(x.h)[-12:] for x in Sem._all[-3:]])
        q = self.q
        self.q = {e: [] for e in self.ENG}
        with nc.Block() as blk:
            def mk(key):
                def body(e):
                    for meth, args, kw, waits, inc in q[key]:
                        for (s, v) in waits:
                            e.wait_ge(s.h, v)
                        ins = getattr(e, meth)(*args, **kw)
                        if inc is not None:
                            ins.then_inc(inc[0].h, inc[1])
                return body
            for key, reg in (("pe", blk.tensor), ("act", blk.scalar), ("dve", blk.vector),
                             ("pool", blk.gpsimd), ("sp", blk.sync)):
                if q[key]:
                    reg(mk(key))


class Arena:
    def __init__(self, nc):
        self.nc = nc
        self.lo = SB_BASE
        self.hi = SB_END
        self.cnt = 0

    def alloc(self, name, shape, dtype, top=False):
        n = 1
        for s in shape[1:]:
            n *= s
        sz = (n * (2 if dtype == BF16 else 4) + 63) // 64 * 64
        if top:
            self.hi -= sz
            off = self.hi
        else:
            off = self.lo
            self.lo += sz
        assert self.lo <= self.hi, "SBUF overflow at %s: lo=%d hi=%d" % (name, self.lo, self.hi)
        self.cnt += 1
        return self.nc.alloc_sbuf_tensor_at("%s_%d" % (name, self.cnt), list(shape), dtype, offset=off)

    def mark(self):
        return (self.lo, self.hi)

    def release(self, m):
        self.lo, self.hi = m


def bc_mid(ap, n):
    return ap.unsqueeze(1).broadcast_to([ap.shape[0], n] + list(ap.shape[1:]))


def bc_last(ap, n):
    return ap.unsqueeze(len(ap.shape)).broadcast_to(list(ap.shape) + [n])


class Ctx:
    pass


def build_nc(debug=False, stop_after=None, mla_heads=8, mla_att=True, mla_stage=3, mla_q=9, skip=()):
    nc = bass.Bass("TRN2", target_bir_lowering=False)
    Sem._all = []
    C = Ctx()
    C.nc = nc

    def din(name, shape, dt=F32):
        return nc.dram_tensor(name, list(shape), dt, kind="ExternalInput").ap()

    xk = din("xk", [NTK, D])
    xq = din("xq", [NQ, D])
    cT_d = din("cT", [128, KC, 2])
    w_ada = din("w_ada", [D, 6 * D])
    b_adaT_d = din("b_adaT", [128, 96])
    b_ada_g = din("b_ada_g", [128, 2 * D])
    n1gT_d = din("n1gT", [128, KC])
    n2gT_d = din("n2gT", [128, KC])
    fin_bc_d = din("fin_bc", [128, D])
    w_in = din("w_in", [D, 8000])
    lam_d = din("lam_in", [128, 256])
    subln_d = din("subln_bc", [128, 128])
    qg_d = din("qg_bc", [128, 512])
    kvg_d = din("kvg_bc", [128, 256])
    w_uq = din("w_uq", [512, 1536])
    w_ukv = din("w_ukv", [256, 2048])
    w_o_da = din("w_o_da", [1024, D])
    w_o_mla = din("w_o_mla", [1024, D])
    w_out = din("w_out", [D, D])
    w_gate = din("w_gate", [D, FH])
    w_up = din("w_up", [D, FH])
    w_down = din("w_down", [FH, D])
    ropeK = din("ropeK", [NTK, 128])
    ropeQ = din("ropeQ", [NQ, 128])
    ident_d = din("ident", [128, 128])
    out_d = nc.dram_tensor("out", [NQ, D], F32, kind="ExternalOutput").ap()

    dbg = {}

    def dscr(name, shape, dt=F32, force_out=False):
        kind = "ExternalOutput" if (debug or force_out) else "Internal"
        t = nc.dram_tensor(name, list(shape), dt, kind=kind).ap()
        dbg[name] = t
        return t

    KTd = dscr("KTd", [8, 128, NTK], BF16)
    Vd = dscr("Vd", [NTK, 1024], BF16)
    g_d = dscr("g_rows", [128, 2 * D], F32)
    if debug:
        dbg_modT = dscr("dbg_modT", [128, 96, 2])
        dbg_ckvnT = dscr("dbg_ckvnT", [128, 2, NTK], BF16)
        dbg_krT = dscr("dbg_krT", [128, NTK], BF16)
        dbg_cqnT = dscr("dbg_cqnT", [128, 4, NQ], BF16)
        dbg_knT = dscr("dbg_knT", [128, NTK], BF16)
        dbg_Vm = dscr("dbg_Vm", [128, NKT, 132], BF16)
        dbg_qnT = dscr("dbg_qnT", [128, NQ], BF16)
        dbg_qrT = dscr("dbg_qrT", [128, NQ], BF16)
        dbg_oTm = dscr("dbg_oTm", [128, 8, NQ], BF16)
        dbg_oTd = dscr("dbg_oTd", [128, 8, NQ], BF16)

    P = Prog(nc)
    import os as _os
    for _i in range(int(_os.environ.get("DUMMYSEM", "0"))):
        P.sem("dummy")
    A = Arena(nc)
    ps = nc.alloc_psum_tensor("ps", [128, 4096], F32)
    psb = ps[:, :].bitcast(BF16)

    def bank(b, n=512, off=0):
        return ps[:, b * 512 + off: b * 512 + off + n]

    def bankb(b, n=1024, off=0):
        return psb[:, b * 1024 + off: b * 1024 + off + n]

    s_done = P.sem("done")
    done_toks = []

    ident = A.alloc("ident", [128, 128], BF16)
    modT = A.alloc("modT", [128, 96, 2], F32)
    A1 = A.alloc("A1", [128, KC, 2], F32)
    A2 = A.alloc("A2", [128, KC], F32)
    n1gT = A.alloc("n1gT", [128, KC], F32)
    n2gT = A.alloc("n2gT", [128, KC], F32)
    eps_t = A.alloc("eps", [128, 1], F32)
    lamt = A.alloc("lamt", [128, 8], F32)
    subln = A.alloc("subln", [128, 128], F32)
    qg_bc = A.alloc("qg", [128, 512], F32)
    kvg_bc = A.alloc("kvg", [128, 256], F32)
    B1 = modT[:, 0:16, :]
    B2 = modT[:, 48:64, 0]

    def phase_A():
        m = A.mark()
        cT = A.alloc("cT", [128, KC, 2], F32)
        s2f = A.alloc("s2f", [128, KC, 2], F32)
        s2b = A.alloc("s2b", [128, KC, 2], BF16)
        srep = A.alloc("srep", [128, KC, 128], BF16)
        b_adaT = A.alloc("b_adaT", [128, 96], F32)
        lam_in = A.alloc("lam_in", [128, 256], F32)
        lam_pr = A.alloc("lam_pr", [128, 256], F32)
        NSL = 3
        slabs = [A.alloc("adaslab", [128, KC, 512], BF16) for _ in range(NSL)]
        gtmp = [A.alloc("gtmp", [128, 512], F32) for _ in range(2)]
        gbias = [A.alloc("gbias", [128, 512], F32) for _ in range(2)]
        s_c = P.sem("constld")
        s_ci = P.sem("identld", sw=True)
        t_c = None
        t_ci = P.op("pool", "dma_start", out=ident[:, :], in_=ident_d, dsem=s_ci)
        for (dst, src, eng) in [(cT, cT_d, "sp"), (b_adaT, b_adaT_d, "sp"),
                                (n1gT, n1gT_d, "sp"), (n2gT, n2gT_d, "sp"), (lam_in, lam_d, "sp"),
                                (subln, subln_d, "sp"), (qg_bc, qg_d, "sp"), (kvg_bc, kvg_d, "sp")]:
            t_c = P.op(eng, "dma_start", out=dst[tuple(slice(None) for _ in dst.shape)], in_=src, dsem=s_c)
        P.op("dve", "memset", eps_t[:, :], EPS, waits=[t_c, t_ci])
        t_silu = P.op("act", "activation", out=s2f[:, :, :], in_=cT[:, :, :], func=AF.Silu, waits=[t_c], sig=True)
        P.op("dve", "tensor_copy", out=s2b[:, :, :], in_=s2f[:, :, :], waits=[t_silu])
        t_srep = P.op("dve", "tensor_copy", out=srep[:, :, :], in_=bc_last(s2f[:, :, 0], 128), sig=True)
        P.op("dve", "tensor_tensor", out=lam_pr[:, 0:64], in0=lam_in[:, 0:64], in1=lam_in[:, 64:128], op=ALU.mult)
        P.op("dve", "tensor_tensor", out=lam_pr[:, 64:128], in0=lam_in[:, 128:192], in1=lam_in[:, 192:256],
             op=ALU.mult)
        t_lp = P.op("dve", "tensor_reduce", out=lamt[:, 1:3],
                    in_=lam_pr[:, 0:128].rearrange("p (a b) -> p a b", a=2), axis=AXX, op=ALU.add, dep=True, sig=True)
        t_le = P.op("act", "activation", out=lamt[:, 3:5], in_=lamt[:, 1:3], func=AF.Exp, waits=[t_lp], sig=True)
        P.op("dve", "tensor_tensor", out=lamt[:, 5:6], in0=lamt[:, 4:5], in1=lamt[:, 3:4], op=ALU.subtract,
             waits=[t_le])
        P.op("dve", "tensor_scalar", out=lamt[:, 0:1], in0=lamt[:, 5:6], scalar1=-LAM_INIT, scalar2=None,
             op0=ALU.add, dep=True)

        s_slab = [P.sem("slab", sw=True) for _ in range(NSL)]
        s_gb = [P.sem("gb") for _ in range(2)]
        s_gst = [P.sem("gst") for _ in range(2)]
        adaview = w_ada.rearrange("(k p) n -> p k n", p=128)
        slab_free = []
        g_hist = []
        gi = 0
        for cb in range(24):
            sl = cb % NSL
            t_ld = P.op("pool", "dma_start", out=slabs[sl][:, :, :], in_=adaview[:, :, cb * 512:(cb + 1) * 512],
                        waits=[slab_free[cb - NSL]] if cb >= NSL else [], dsem=s_slab[sl])
            isg = cb in (8, 9, 10, 11, 20, 21, 22, 23)
            t_last = None
            for c4 in range(4):
                ch = cb * 4 + c4
                for k in range(KC):
                    last = (c4 == 3 and k == KC - 1)
                    t_last = P.op("pe", "matmul", out=bank(0, 2, ch * 2),
                                  lhsT=slabs[sl][:, k, c4 * 128:(c4 + 1) * 128], rhs=s2b[:, k, :],
                                  start=(k == 0), stop=(k == KC - 1), waits=[t_ld, t_srep], sig=last)
            if isg:
                gb_ = 1 + gi % 2
                wv = [g_hist[gi - 2][0]] if gi >= 2 else []
                for k in range(KC):
                    t_last = P.op("pe", "matmul", out=bank(gb_), lhsT=srep[:, k, :], rhs=slabs[sl][:, k, :],
                                  start=(k == 0), stop=(k == KC - 1), waits=wv, sig=(k == KC - 1))
                gcol = (cb - 8) * 512 if cb < 12 else D + (cb - 20) * 512
                t_b = P.op("sp", "dma_start", out=gbias[gi % 2][:, :], in_=b_ada_g[:, gcol:gcol + 512],
                           waits=wv, dsem=s_gb[gi % 2])
                t_d = P.op("dve", "tensor_tensor", out=gtmp[gi % 2][:, :], in0=bank(gb_), in1=gbias[gi % 2][:, :],
                           op=ALU.add, waits=[t_last, t_b] + ([g_hist[gi - 2][1]] if gi >= 2 else []), sig=True)
                t_s = P.op("sp", "dma_start", out=g_d[:, gcol:gcol + 512], in_=gtmp[gi % 2][:, :], waits=[t_d],
                           dsem=s_gst[gi % 2])
                g_hist.append((t_d, t_s))
                gi += 1
            slab_free.append(t_last)
        P.op("dve", "tensor_tensor", out=modT[:, :, :], in0=bank(0, 192).rearrange("p (a b) -> p a b", b=2),
             in1=bc_last(b_adaT[:, :], 2), op=ALU.add, waits=[slab_free[-1]])
        P.op("dve", "tensor_scalar", out=A1[:, :, :], in0=modT[:, 16:32, :], scalar1=1.0, scalar2=None, op0=ALU.add,
             dep=True)
        P.op("dve", "tensor_tensor", out=A1[:, :, :], in0=A1[:, :, :], in1=bc_last(n1gT[:, :], 2), op=ALU.mult,
             dep=True)
        P.op("dve", "tensor_scalar", out=A2[:, :], in0=modT[:, 64:80, 0], scalar1=1.0, scalar2=None, op0=ALU.add)
        t_A = P.op("dve", "tensor_tensor", out=A2[:, :], in0=A2[:, :], in1=n2gT[:, :], op=ALU.mult, dep=True,
                   sig=True)
        for h in g_hist[-2:]:
            P.op("sp", "wait_ge", h[1][0].h, h[1][1])
        if debug:
            done_toks.append(P.op("sp", "dma_start", out=dbg_modT, in_=modT[:, :, :], waits=[t_A], dsem=s_done))
            P.op("sp", "wait_ge", s_done.h, s_done.n)
        P.flush()
        A.release(m)

    class NormPipe:
        def __init__(self, name, tb, nh, src_fn, ab_fn, NX=2, xt=None, out_fn=None):
            self.tb = tb
            self.NX = NX
            self.nh = nh
            self.src_fn = src_fn
            self.ab_fn = ab_fn
            self.out_fn = out_fn
            self.xt = xt if xt is not None else [A.alloc(name + "xt", [128, D], F32) for _ in range(NX)]
            self.xn = [A.alloc(name + "xn", [128, D], BF16) for _ in range(2)]
            self.tmp = [A.alloc(name + "tmp", [128, 8, 128], F32) for _ in range(2)]
            self.hT = [A.alloc(name + "hT", [128, KC, 128], BF16) for _ in range(nh)] if out_fn is None else None
            self.st = A.alloc(name + "st", [128, 8], F32)
            self.s_x = [P.sem(name + "x") for _ in range(NX)]
            self.t_load = {}
            self.t_xfree = {}
            self.t_xn = {}
            self.t_tr = {}
            self.t_mul = {}
            self.t_add = {}
            self.t_hT = {}
            self.hT_rel = {}
            self.t_trdone = {}

        def load(self, i, extra_waits=()):
            sl = i % self.NX
            self.t_load[i] = P.op("sp", "dma_start", out=self.xt[sl][:, :], in_=self.src_fn(i),
                                  waits=[self.t_xfree.get(i - self.NX)] + list(extra_waits), dsem=self.s_x[sl])

        def stats(self, i, src=None, src_wait=None):
            x = self.xt[i % self.NX][:, :] if src is None else src
            w0 = [self.t_load[i]] if src is None else [src_wait]
            c = i % 2
            st = self.st
            P.op("act", "activation", out=self.xn[c][:, :], in_=x, func=AF.Square, accum_out=st[:, c:c + 1],
                 waits=w0 + [self.t_xn.get(i - 2), self.t_trdone.get(i - 2)])
            t_sq = P.op("act", "activation", out=st[:, 2 + c:3 + c], in_=st[:, c:c + 1], func=AF.Sqrt,
                        scale=1.0 / D, bias=eps_t[:, :], dep=True, sig=True)
            P.op("dve", "reciprocal", out=st[:, 4 + c:5 + c], in_=st[:, 2 + c:3 + c], waits=[t_sq])
            self.t_xn[i] = P.op("dve", "tensor_scalar", out=self.xn[c][:, :], in0=x, scalar1=st[:, 4 + c:5 + c],
                                scalar2=None, op0=ALU.mult, dep=True, waits=w0 + [self.t_trdone.get(i - 2)],
                                sig=True)
            self.t_xfree[i] = [self.t_xn[i]]

        def transposes(self, i):
            c = i % 2
            for h in range(2):
                wv = [self.t_xn[i], self.t_mul.get((i - 1, h))]
                for k in range(8):
                    kk = h * 8 + k
                    t = P.op("pe", "transpose", out=bankb(self.tb[h], 128, k * 128),
                             in_=self.xn[c][:, kk * 128:(kk + 1) * 128], identity=ident[:, :], waits=wv,
                             sig=(k == 7))
                self.t_tr[(i, h)] = t
            self.t_trdone[i] = t

        def evac(self, i):
            Aap, Bap = self.ab_fn(i)
            sl = i % self.nh
            dst = self.hT[sl] if self.out_fn is None else self.out_fn(i)
            for h in range(2):
                self.t_mul[(i, h)] = P.op(
                    "dve", "tensor_tensor", out=self.tmp[h][:, :, :],
                    in0=bankb(self.tb[h]).rearrange("p (a b) -> p a b", a=8),
                    in1=bc_last(Aap[:, h * 8:(h + 1) * 8], 128), op=ALU.mult,
                    waits=[self.t_tr[(i, h)], self.t_add.get((i - 1, h))], sig=True)
                self.t_add[(i, h)] = P.op(
                    "pool", "tensor_tensor", out=dst[:, h * 8:(h + 1) * 8, :], in0=self.tmp[h][:, :, :],
                    in1=bc_last(Bap[:, h * 8:(h + 1) * 8], 128), op=ALU.add,
                    waits=[self.t_mul[(i, h)], self.hT_rel.get(i - self.nh)], sig=True)
            self.t_hT[i] = self.t_add[(i, 1)]

    def rope_ops(src3, nsub, cs, tC, tS, dst2, waits):
        n = nsub * 64
        P.op("dve", "tensor_tensor", out=tC[:, 0:n].rearrange("p (s d) -> p s d", s=nsub), in0=src3,
             in1=bc_mid(cs[:, 0:64], nsub), op=ALU.mult, waits=waits)
        src5 = src3.rearrange("p s (h x j) -> p s h x j", h=2, x=2)
        tS5 = tS[:, 0:n].rearrange("p (s h x j) -> p s h x j", s=nsub, h=2, x=2)
        S4 = cs[:, 64:128].rearrange("p (h x j) -> p h x j", h=2, x=2)
        P.op("dve", "tensor_tensor", out=tS5[:, :, :, 0, :], in0=src5[:, :, :, 1, :],
             in1=bc_mid(S4[:, :, 0, :], nsub), op=ALU.mult)
        t_rd = P.op("dve", "tensor_tensor", out=tS5[:, :, :, 1, :], in0=src5[:, :, :, 0, :],
                    in1=bc_mid(S4[:, :, 1, :], nsub), op=ALU.mult, sig=True)
        return t_rd

    mK = A.mark()
    ckvnT = A.alloc("ckvnT", [128, 2, NTK], BF16, top=True)
    krT2 = A.alloc("krT2", [128, NTK], BF16, top=True)

    def phase_K():
        m = A.mark()
        wkv = A.alloc("wkv", [128, KC, 2368], BF16)
        s_w = P.sem("wkv", sw=True)
        w3 = w_in.rearrange("(k p) n -> p k n", p=128)
        t_w = None
        for kq in range(4):
            P.op("pool", "dma_start", out=wkv[:, kq * 4:(kq + 1) * 4, 0:2048], in_=w3[:, kq * 4:(kq + 1) * 4, 1024:3072],
                 dsem=s_w)
            t_w = P.op("pool", "dma_start", out=wkv[:, kq * 4:(kq + 1) * 4, 2048:2368],
                       in_=w3[:, kq * 4:(kq + 1) * 4, 3584:3904], dsem=s_w)

        def ab(i):
            s = 1 if i < 2 else 0
            return A1[:, :, s], B1[:, :, s]
        NP = NormPipe("k", (0, 1), 2, lambda i: xk[i * 128:(i + 1) * 128, :], ab, NX=3)
        cs = [A.alloc("kcs", [128, 128], F32) for _ in range(2)]
        s_cs = [P.sem("kcs") for _ in range(2)]
        tC = A.alloc("ktC", [128, 1024], F32)
        tS = A.alloc("ktS", [128, 1024], F32)
        tC2 = A.alloc("ktC2", [128, 64], F32)
        tS2 = A.alloc("ktS2", [128, 64], F32)
        krot = [A.alloc("krot", [128, 1024], BF16) for _ in range(2)]
        krr = [A.alloc("krr", [128, 128], BF16) for _ in range(2)]
        ckvn = [A.alloc("ckvn", [128, 256], BF16) for _ in range(2)]
        kst = [A.alloc("kst", [128, 8, 256], BF16) for _ in range(2)]
        vst = [A.alloc("vst", [128, 1024], BF16) for _ in range(2)]
        st2 = A.alloc("kst2", [128, 8], F32)
        junk2 = A.alloc("kjunk2", [128, 256], BF16)
        s_kst = [P.sem("kstd") for _ in range(2)]
        s_vst = [P.sem("vstd") for _ in range(2)]
        KTv = KTd.rearrange("h p t -> p h t")
        T = {}

        def proj_U1(i):
            hT = NP.hT[i % 2]
            wv = [NP.t_hT[i], t_w, T.get(("rope_rd", i - 1)), T.get(("vcopy", i - 1))]
            for k in range(KC):
                for j in range(4):
                    t = P.op("pe", "matmul", out=bank(2 + j), lhsT=hT[:, k, :], rhs=wkv[:, k, j * 512:(j + 1) * 512],
                             start=(k == 0), stop=(k == KC - 1), waits=wv, sig=(k == KC - 1 and j == 3))
            T[("U1", i)] = t

        def proj_U2(i):
            hT = NP.hT[i % 2]
            wv = [T.get(("smallev", i - 1)), T.get(("ckvn", i - 1)), T.get(("kr_rd", i - 1)), T.get(("sq2", i - 1))]
            for k in range(KC):
                t = P.op("pe", "matmul", out=bank(6, 320), lhsT=hT[:, k, :], rhs=wkv[:, k, 2048:2368],
                         start=(k == 0), stop=(k == KC - 1), waits=wv, sig=(k == KC - 1))
            T[("U2", i)] = t
            NP.hT_rel[i] = [t]

        def post(i):
            c = i % 2
            t_cs = P.op("sp", "dma_start", out=cs[c][:, :], in_=ropeK[i * 128:(i + 1) * 128, :],
                        waits=[T.get(("rope_rd", i - 2)), T.get(("kr_rd", i - 2))], dsem=s_cs[c])
            t_rd = rope_ops(bank(2, 1024).rearrange("p (s d) -> p s d", s=16), 16, cs[c], tC, tS, krot[c],
                            [T[("U1", i)], t_cs, T.get(("krot_add", i - 1))])
            T[("rope_rd", i)] = t_rd
            T[("krot_add", i)] = P.op("pool", "tensor_tensor", out=krot[c][:, :], in0=tC[:, :], in1=tS[:, :],
                                      op=ALU.add, waits=[t_rd, T.get(("trK", i - 2))], sig=True)
            T[("vcopy", i)] = P.op("act", "activation", out=vst[c][:, :], in_=bank(4, 1024), func=AF.Copy,
                                   waits=[T[("U1", i)], T.get(("vstore", i - 2))], sig=True)
            T[("vstore", i)] = P.op("sp", "dma_start", out=Vd[i * 128:(i + 1) * 128, :], in_=vst[c][:, :],
                                    waits=[T[("vcopy", i)]], dsem=s_vst[c])
            P.op("act", "activation", out=junk2[:, :], in_=bank(6, 256), func=AF.Square, accum_out=st2[:, c:c + 1],
                 waits=[T[("U2", i)]])
            t_sq = P.op("act", "activation", out=st2[:, 2 + c:3 + c], in_=st2[:, c:c + 1], func=AF.Sqrt,
                        scale=1.0 / 256, bias=eps_t[:, :], dep=True, sig=True)
            T[("sq2", i)] = t_sq
            P.op("dve", "reciprocal", out=st2[:, 4 + c:5 + c], in_=st2[:, 2 + c:3 + c], waits=[t_sq])
            T[("ckvn", i)] = P.op("dve", "scalar_tensor_tensor", out=ckvn[c][:, :], in0=bank(6, 256),
                                  scalar=st2[:, 4 + c:5 + c], in1=kvg_bc[:, :], op0=ALU.mult, op1=ALU.mult,
                                  dep=True, waits=[T.get(("smalltr", i - 2))], sig=True)
            t_rd2 = rope_ops(bank(6, 64, 256).rearrange("p (s d) -> p s d", s=1), 1, cs[c], tC2, tS2, None,
                             [T[("U2", i)], t_sq, t_cs, T.get(("krr_add", i - 1))])
            T[("kr_rd", i)] = t_rd2
            P.op("pool", "tensor_tensor", out=krr[c][:, 0:64], in0=tC2[:, :], in1=tS2[:, :], op=ALU.add,
                 waits=[t_rd2, T.get(("smalltr", i - 2))])
            T[("krr_add", i)] = P.op("pool", "tensor_tensor", out=krr[c][:, 64:128], in0=tC2[:, :], in1=tS2[:, :],
                                     op=ALU.add, sig=True)

        def trK(i):
            c = i % 2
            wv = [T[("krot_add", i)], T.get(("KTev", i - 1))]
            for j in range(8):
                t = P.op("pe", "transpose", out=bankb(7, 128, j * 128), in_=krot[c][:, j * 128:(j + 1) * 128],
                         identity=ident[:, :], waits=wv, sig=(j == 7))
            T[("trK", i)] = t
            wv = [T[("ckvn", i)], T[("krr_add", i)], T.get(("smallev", i - 1)), T[("U2", i)], T[("kr_rd", i)],
                  T[("sq2", i)]]
            for j in range(2):
                P.op("pe", "transpose", out=bankb(6, 128, 640 + j * 128), in_=ckvn[c][:, j * 128:(j + 1) * 128],
                     identity=ident[:, :], waits=wv)
            T[("smalltr", i)] = P.op("pe", "transpose", out=bankb(6, 128, 896), in_=krr[c][:, :],
                                     identity=ident[:, :], waits=wv, sig=True)

        def evK(i):
            g = i // 2
            gc = g % 2
            T[("KTev", i)] = P.op("act", "activation", out=kst[gc][:, :, (i % 2) * 128:(i % 2) * 128 + 128],
                                  in_=bankb(7).rearrange("p (a b) -> p a b", a=8), func=AF.Copy,
                                  waits=[T[("trK", i)], T.get(("kstore", g - 2))], sig=True)
            if i % 2 == 1:
                T[("kstore", g)] = P.op("sp", "dma_start", out=KTv[:, :, g * 256:(g + 1) * 256], in_=kst[gc][:, :, :],
                                        waits=[T[("KTev", i)]], dsem=s_kst[gc])
            P.op("dve", "tensor_copy", out=ckvnT[:, :, i * 128:(i + 1) * 128],
                 in_=bankb(6, 256, 640).rearrange("p (a b) -> p a b", a=2), waits=[T[("smalltr", i)]])
            T[("smallev", i)] = P.op("dve", "tensor_copy", out=krT2[:, i * 128:(i + 1) * 128], in_=bankb(6, 128, 896),
                                     sig=True)

        import os as _os
        N = int(_os.environ.get("KN", NKT))
        NP.load(0)
        NP.load(1)
        NP.load(2)
        NP.stats(0)
        NP.stats(1)
        NP.transposes(0)
        NP.evac(0)
        for i in range(N):
            if i + 3 < N:
                NP.load(i + 3)
            if i + 2 < N:
                NP.stats(i + 2)
            if i + 1 < N:
                NP.transposes(i + 1)
                NP.evac(i + 1)
            proj_U1(i)
            if i >= 1:
                trK(i - 1)
                evK(i - 1)
            proj_U2(i)
            post(i)
        trK(N - 1)
        evK(N - 1)
        for s in s_kst + s_vst:
            P.op("sp", "wait_ge", s.h, s.n)
        if debug:
            t1 = P.op("sp", "dma_start", out=dbg_ckvnT, in_=ckvnT[:, :, :], waits=[P.last_tok("dve")], dsem=s_done)
            t2 = P.op("sp", "dma_start", out=dbg_krT, in_=krT2[:, :], dsem=s_done)
            P.op("sp", "wait_ge", s_done.h, s_done.n)
        P.flush()
        A.release(m)


    QTd = dscr("QTd", [128, 8, NQ], BF16)
    Wg_s = nc.dram_tensor("Wg_s", [16, 128, 6144], BF16, kind="Internal").ap()
    Wgu_s = nc.dram_tensor("Wgu_s", [FHC, 128, 4096], BF16, kind="Internal").ap()
    Wd_s = nc.dram_tensor("Wd_s", [16, 128, 5632], BF16, kind="Internal").ap()
    Wo_s = nc.dram_tensor("Wo_s", [8, 128, 4096], BF16, kind="Internal").ap()

    class Stager:
        def __init__(self, chunks, nbuf_elems):
            self.chunks = chunks
            self.bufs = [A.alloc("stg", [128, nbuf_elems], BF16) for _ in range(2)]
            self.s_in = [P.sem("stgi", sw=True) for _ in range(2)]
            self.s_out = [P.sem("stgo", sw=True) for _ in range(2)]
            self.t_out = {}
            self.t_in = {}
            self.c = 0
            if chunks:
                self.do_in(0)

        def do_in(self, c):
            parts, _, _ = self.chunks[c]
            sl = c % 2
            t = None
            for (lo, hi, vf, src) in parts:
                t = P.op("pool", "dma_start", out=vf(self.bufs[sl][:, lo:hi]), in_=src,
                         waits=[self.t_out.get(c - 2)], dsem=self.s_in[sl])
            self.t_in[c] = t

        def step(self, k):
            n = len(self.chunks)
            for _ in range(k):
                c = self.c
                if c >= n:
                    return
                if c + 1 < n:
                    self.do_in(c + 1)
                _, dst, nel = self.chunks[c]
                sl = c % 2
                self.t_out[c] = P.op("pool", "dma_start", out=dst, in_=self.bufs[sl][:, 0:nel],
                                     waits=[self.t_in[c]], dsem=self.s_out[sl])
                self.c += 1

        def finish(self):
            self.step(len(self.chunks))
            for s_ in self.s_out:
                P.op("pool", "wait_ge", s_.h, s_.n)

    w3in = w_in.rearrange("(k p) n -> p k n", p=128)
    woa3 = w_o_da.rearrange("(k p) n -> p k n", p=128)
    wob3 = w_o_mla.rearrange("(k p) n -> p k n", p=128)
    wg3 = w_gate.rearrange("(k p) n -> p k n", p=128)
    wu3 = w_up.rearrange("(k p) n -> p k n", p=128)
    wd3 = w_down.rearrange("(k p) n -> p k n", p=128)
    v16 = lambda ap: ap.rearrange("p (k c) -> p k c", k=16)
    v8 = lambda ap: ap.rearrange("p (k c) -> p k c", k=8)
    v22 = lambda ap: ap.rearrange("p (k c) -> p k c", k=22)
    chunks_g = []
    for fo in range(16):
        c0 = fo * 128
        chunks_g.append(([(0, 2048, v16, w3in[:, :, 3904 + c0:3904 + c0 + 128]),
                          (2048, 4096, v16, w3in[:, :, 5952 + c0:5952 + c0 + 128]),
                          (4096, 5120, v8, woa3[:, :, c0:c0 + 128]),
                          (5120, 6144, v8, wob3[:, :, c0:c0 + 128])], Wg_s[fo, :, :], 6144))
    chunks_gu = []
    for hc in range(FHC):
        c0 = hc * 128
        chunks_gu.append(([(0, 2048, v16, wg3[:, :, c0:c0 + 128]), (2048, 4096, v16, wu3[:, :, c0:c0 + 128])],
                          Wgu_s[hc, :, :], 4096))
    chunks_o = []
    wout3s = w_out.rearrange("(k p) n -> p k n", p=128)
    for cbh in range(8):
        chunks_o.append(([(0, 4096, v16, wout3s[:, :, cbh * 256:(cbh + 1) * 256])], Wo_s[cbh, :, :], 4096))
    chunks_d = []
    for half in range(2):
        for cb in range(8):
            chunks_d.append(([(0, 5632, v22, wd3[:, half * 22:(half + 1) * 22, cb * 256:(cb + 1) * 256])],
                             Wd_s[half * 8 + cb, :, :], 5632))

    def phase_Q():
        m = A.mark()
        wq = A.alloc("wq", [128, KC, 1536], BF16)
        s_w = P.sem("wq", sw=True)
        w3 = w_in.rearrange("(k p) n -> p k n", p=128)
        t_w = None
        for kq in range(4):
            P.op("pool", "dma_start", out=wq[:, kq * 4:(kq + 1) * 4, 0:1024], in_=w3[:, kq * 4:(kq + 1) * 4, 0:1024],
                 dsem=s_w)
            t_w = P.op("pool", "dma_start", out=wq[:, kq * 4:(kq + 1) * 4, 1024:1536],
                       in_=w3[:, kq * 4:(kq + 1) * 4, 3072:3584], dsem=s_w)
        NP = NormPipe("q", (0, 1), 2, lambda i: xq[i * 128:(i + 1) * 128, :], lambda i: (A1[:, :, 0], B1[:, :, 0]), NX=3)
        cs = [A.alloc("qcs", [128, 128], F32) for _ in range(2)]
        s_cs = [P.sem("qcs") for _ in range(2)]
        tC = A.alloc("qtC", [128, 1024], F32)
        tS = A.alloc("qtS", [128, 1024], F32)
        qrot = [A.alloc("qrot", [128, 1024], BF16) for _ in range(2)]
        cqn = [A.alloc("cqn", [128, 512], BF16) for _ in range(2)]
        st2 = A.alloc("qst2", [128, 8], F32)
        junk2 = A.alloc("qjunk2", [128, 512], BF16)
        qst = [A.alloc("qst", [128, 8, 256], BF16) for _ in range(2)]
        s_qst = [P.sem("qstd") for _ in range(2)]
        T = {}

        def proj(i):
            hT = NP.hT[i % 2]
            wv = [NP.t_hT[i], t_w, T.get(("rope_rd", i - 1)), T.get(("cqn", i - 1)), T.get(("sq2", i - 1))]
            for k in range(KC):
                for j in range(3):
                    t = P.op("pe", "matmul", out=bank(2 + j), lhsT=hT[:, k, :], rhs=wq[:, k, j * 512:(j + 1) * 512],
                             start=(k == 0), stop=(k == KC - 1), waits=wv, sig=(k == KC - 1 and j == 2))
            T[("U", i)] = t
            NP.hT_rel[i] = [t]

        def post(i):
            c = i % 2
            t_cs = P.op("sp", "dma_start", out=cs[c][:, :], in_=ropeQ[i * 128:(i + 1) * 128, :],
                        waits=[T.get(("rope_rd", i - 2))], dsem=s_cs[c])
            t_rd = rope_ops(bank(2, 1024).rearrange("p (s d) -> p s d", s=16), 16, cs[c], tC, tS, None,
                            [T[("U", i)], t_cs, T.get(("qrot_add", i - 1))])
            T[("rope_rd", i)] = t_rd
            T[("qrot_add", i)] = P.op("pool", "tensor_tensor", out=qrot[c][:, :], in0=tC[:, :], in1=tS[:, :],
                                      op=ALU.add, waits=[t_rd, T.get(("trQ", i - 2))], sig=True)
            P.op("act", "activation", out=junk2[:, :], in_=bank(4, 512), func=AF.Square, accum_out=st2[:, c:c + 1],
                 waits=[T[("U", i)]])
            t_sq = P.op("act", "activation", out=st2[:, 2 + c:3 + c], in_=st2[:, c:c + 1], func=AF.Sqrt,
                        scale=1.0 / 512, bias=eps_t[:, :], dep=True, sig=True)
            T[("sq2", i)] = t_sq
            P.op("dve", "reciprocal", out=st2[:, 4 + c:5 + c], in_=st2[:, 2 + c:3 + c], waits=[t_sq])
            T[("cqn", i)] = P.op("dve", "scalar_tensor_tensor", out=cqn[c][:, :], in0=bank(4, 512),
                                 scalar=st2[:, 4 + c:5 + c], in1=qg_bc[:, :], op0=ALU.mult, op1=ALU.mult,
                                 dep=True, waits=[T.get(("trQ", i - 2))], sig=True)

        def trQ(i):
            c = i % 2
            wv = [T[("qrot_add", i)], T.get(("Qev", i - 1))]
            for j in range(8):
                P.op("pe", "transpose", out=bankb(7, 128, j * 128), in_=qrot[c][:, j * 128:(j + 1) * 128],
                     identity=ident[:, :], waits=wv)
            wv = [T[("cqn", i)], T.get(("Qev", i - 1))]
            for j in range(4):
                t = P.op("pe", "transpose", out=bankb(6, 128, j * 128), in_=cqn[c][:, j * 128:(j + 1) * 128],
                         identity=ident[:, :], waits=wv, sig=(j == 3))
            T[("trQ", i)] = t

        def evQ(i):
            g = i // 2
            gc = g % 2
            P.op("act", "activation", out=qst[gc][:, :, (i % 2) * 128:(i % 2) * 128 + 128],
                 in_=bankb(7).rearrange("p (a b) -> p a b", a=8), func=AF.Copy,
                 waits=[T[("trQ", i)], T.get(("qstore", g - 2))])
            t1 = P.last_tok("act")
            if i % 2 == 1:
                T[("qstore", g)] = P.op("sp", "dma_start", out=QTd[:, :, g * 256:(g + 1) * 256], in_=qst[gc][:, :, :],
                                        waits=[t1], dsem=s_qst[gc])
            t2 = P.op("dve", "tensor_copy", out=cqnT[:, :, i * 128:(i + 1) * 128],
                      in_=bankb(6, 512).rearrange("p (a b) -> p a b", a=4), waits=[T[("trQ", i)]], sig=True)
            T[("Qev", i)] = [t1, t2]

        import os as _os
        N = int(_os.environ.get("QN", NQT))
        NP.load(0)
        NP.load(1)
        NP.load(2)
        NP.stats(0)
        NP.stats(1)
        NP.transposes(0)
        NP.evac(0)
        for i in range(N):
            if i + 3 < N:
                NP.load(i + 3)
            if i + 2 < N:
                NP.stats(i + 2)
            if i + 1 < N:
                NP.transposes(i + 1)
                NP.evac(i + 1)
            proj(i)
            if i >= 1:
                trQ(i - 1)
                evQ(i - 1)
            post(i)
        trQ(N - 1)
        evQ(N - 1)
        for s_ in s_qst:
            P.op("sp", "wait_ge", s_.h, s_.n)
        if debug:
            P.op("sp", "dma_start", out=dbg_cqnT, in_=cqnT[:, :, :], waits=[T[("Qev", N - 1)]], dsem=s_done)
            P.op("sp", "wait_ge", s_done.h, s_done.n)
        P.flush()
        A.release(m)


    def attention(name, nheads, head_setup, st_mms, scale, finish):
        pass

    def phase_MLA():
        m = A.mark()
        ropeQt = A.alloc("ropeQt", [128, NQT, 128], F32)
        wkvh = [A.alloc("wkvh", [128, 2, 256], BF16) for _ in range(2)]
        wuqh = [A.alloc("wuqh", [128, 4, 192], BF16) for _ in range(2)]
        knT = A.alloc("knT", [128, NTK], BF16)
        Vm = A.alloc("Vm", [128, NKT, 132], BF16)
        qnT = A.alloc("qnT", [128, NQ], BF16)
        qrT2 = A.alloc("qrT2", [128, NQ], BF16)
        qn_tok = [A.alloc("qn_tok", [128, 128], BF16) for _ in range(2)]
        qr_tok = [A.alloc("qr_tok", [128, 128], BF16) for _ in range(2)]
        tC2 = A.alloc("mtC2", [128, 64], F32)
        tS2 = A.alloc("mtS2", [128, 64], F32)
        NPT = 3
        PT = [A.alloc("mPT", [128, 1024], BF16) for _ in range(NPT)]
        oacc = A.alloc("moacc", [128, 4, 132], F32)
        rec = A.alloc("mrec", [128, 4], F32)
        o_tok = A.alloc("mo_tok", [128, 4, 128], BF16)
        s_w = [P.sem("mw", sw=True) for _ in range(2)]
        s_r = P.sem("mrope")
        t_rq = P.op("sp", "dma_start", out=ropeQt[:, :, :], in_=ropeQ.rearrange("(t p) c -> p t c", p=128), dsem=s_r)
        P.op("dve", "memset", Vm[:, :, 128:129], 1.0)
        t_ones = P.last_tok("dve")
        wkv3 = w_ukv.rearrange("(k p) n -> p k n", p=128)
        wuq3 = w_uq.rearrange("(k p) n -> p k n", p=128)
        T = {}
        stg = Stager(chunks_g + chunks_o + chunks_gu, 6144)
        pending = []
        STG = (0, 7)
        SP_ = ((1, 2), (3, 4))
        AB = (5, 6)
        cnt = {"st": 0}
        st_hist = {}
        NPP = NKT // 2
        for h in range(mla_heads):
            c = h % 2
            P.op("pool", "dma_start", out=wkvh[c][:, :, :], in_=wkv3[:, :, h * 256:(h + 1) * 256],
                 waits=[T.get(("wfree", h - 2))], dsem=s_w[c])
            t_wh = P.op("pool", "dma_start", out=wuqh[c][:, :, :], in_=wuq3[:, :, h * 192:(h + 1) * 192], dsem=s_w[c])
            stg.step(9)
            prev_att = [T.get(("att_done", h - 1)), T.get(("oev", h - 1, 3))]
            for blk in range(17):
                t0 = blk * 512
                n = 512 if blk < 16 else 256
                bk = STG[blk % 2]
                wv = [t_wh, T.get(("knev", h, blk - 2)), prev_att]
                for k in range(2):
                    t = P.op("pe", "matmul", out=bank(bk, n), lhsT=wkvh[c][:, k, 0:128], rhs=ckvnT[:, k, t0:t0 + n],
                             start=(k == 0), stop=(k == 1), waits=wv, sig=(k == 1))
                if blk % 2 == 0:
                    P.op("act", "activation", out=knT[:, t0:t0 + n], in_=bank(bk, n), func=AF.Copy, waits=[t])
                    T[("knev", h, blk)] = P.last_tok("act")
                else:
                    T[("knev", h, blk)] = P.op("dve", "tensor_copy", out=knT[:, t0:t0 + n], in_=bank(bk, n),
                                               waits=[t], sig=True)
            for g in range(17):
                bk = STG[(g + 1) % 2]
                nt = 4 if g < 16 else 2
                wv = [T.get(("vev", h, g - 2)), T.get(("knev", h, 16)), T.get(("knev", h, 15)), t_ones, prev_att]
                for tt in range(nt):
                    kt = g * 4 + tt
                    for k in range(2):
                        t = P.op("pe", "matmul", out=bank(bk, 128, tt * 128), lhsT=ckvnT[:, k, kt * 128:(kt + 1) * 128],
                                 rhs=wkvh[c][:, k, 128:256], start=(k == 0 and tt == 0), stop=(k == 1),
                                 skip_group_check=True, waits=wv, sig=(k == 1 and tt == nt - 1))
                src = bank(bk, nt * 128).rearrange("p (a b) -> p a b", a=nt)
                if g % 2 == 0:
                    T[("vev", h, g)] = P.op("dve", "tensor_copy", out=Vm[:, g * 4:g * 4 + nt, 0:128], in_=src,
                                            waits=[t], sig=True)
                else:
                    P.op("act", "activation", out=Vm[:, g * 4:g * 4 + nt, 0:128], in_=src, func=AF.Copy, waits=[t])
                    T[("vev", h, g)] = P.last_tok("act")
            for i in range(NQT):
                bk = STG[i % 2]
                wv = [T.get(("qev", h, i - 2)), T.get(("vev", h, 16)), T.get(("vev", h, 15)), prev_att]
                for k in range(4):
                    t = P.op("pe", "matmul", out=bank(bk, 192), lhsT=cqnT[:, k, i * 128:(i + 1) * 128],
                             rhs=wuqh[c][:, k, :], start=(k == 0), stop=(k == 3), waits=wv, sig=(k == 3))
                ci = i % 2
                P.op("act", "activation", out=qn_tok[ci][:, :], in_=bank(bk, 128), func=AF.Copy,
                     waits=[t, T.get(("qtr", h, i - 2))])
                t_qn = P.last_tok("act")
                t_rd = rope_ops(bank(bk, 64, 128).rearrange("p (s d) -> p s d", s=1), 1, ropeQt[:, i, :], tC2, tS2,
                                None, [t, t_qn, t_rq])
                P.op("dve", "tensor_tensor", out=qr_tok[ci][:, 0:64], in0=tC2[:, :], in1=tS2[:, :], op=ALU.add,
                     dep=True, waits=[T.get(("qtr", h, i - 2))])
                t_add = P.op("dve", "tensor_tensor", out=qr_tok[ci][:, 64:128], in0=tC2[:, :], in1=tS2[:, :],
                             op=ALU.add, sig=True)
                wv2 = [t_qn, t_add]
                P.op("pe", "transpose", out=bankb(bk, 128, 512), in_=qn_tok[ci][:, :], identity=ident[:, :], waits=wv2)
                t_tr = P.op("pe", "transpose", out=bankb(bk, 128, 640), in_=qr_tok[ci][:, :], identity=ident[:, :],
                            sig=True)
                T[("qtr", h, i)] = t_tr
                P.op("dve", "tensor_copy", out=qnT[:, i * 128:(i + 1) * 128], in_=bankb(bk, 128, 512), waits=[t_tr])
                t_e = P.op("dve", "tensor_copy", out=qrT2[:, i * 128:(i + 1) * 128], in_=bankb(bk, 128, 640), sig=True)
                T[("qev", h, i)] = [t_qn, t_add, t_e]
            t_qdone = T[("qev", h, NQT - 1)]
            for qb in (range(4) if mla_att else []):
                q0 = qb * 512
                wv_acc = [T.get(("accev", h, qb - 1)) if qb > 0 else T.get(("accev", h - 1, 3))]

                def emit_S(pp):
                    n = cnt["st"]
                    cnt["st"] += 1
                    p = SP_[n % 2]
                    wv = [t_qdone, st_hist.get(n - 2, {}).get("exp")]
                    for j in range(2):
                        kt = 2 * pp + j
                        P.op("pe", "matmul", out=bank(p[j]), lhsT=knT[:, kt * 128:(kt + 1) * 128],
                             rhs=qnT[:, q0:q0 + 512], start=True, stop=False, waits=wv)
                    for j in range(2):
                        kt = 2 * pp + j
                        t_s = P.op("pe", "matmul", out=bank(p[j]), lhsT=krT2[j * 64:(j + 1) * 64, kt * 128:(kt + 1) * 128],
                                   rhs=qrT2[j * 64:(j + 1) * 64, q0:q0 + 512], start=False, stop=True, sig=(j == 1))
                    st_hist[n] = {"s": t_s}
                    return n
                n_next = emit_S(0)
                for pp in range(NPP):
                    n = n_next
                    if pp + 1 < NPP:
                        n_next = emit_S(pp + 1)
                    p = SP_[n % 2]
                    pt = PT[n % NPT]
                    t_e = P.op("act", "activation", out=pt[:, :], in_=ps[:, p[0] * 512:p[0] * 512 + 1024], func=AF.Exp,
                               scale=MLA_SCALE, waits=[st_hist[n]["s"], st_hist.get(n - NPT, {}).get("pv")], sig=True)
                    st_hist[n]["exp"] = t_e
                    for j in range(2):
                        kt = 2 * pp + j
                        for qt in range(4):
                            t_pv = P.op("pe", "matmul", out=bank(AB[qt // 2], 129, (qt % 2) * 132),
                                        lhsT=pt[:, j * 512 + qt * 128: j * 512 + (qt + 1) * 128], rhs=Vm[:, kt, 0:129],
                                        start=(pp == 0 and j == 0 and qt % 2 == 0), stop=(pp == NPP - 1 and j == 1),
                                        skip_group_check=True, waits=[t_e] + (wv_acc if (pp == 0 and j == 0) else []),
                                        sig=(j == 1 and qt == 3))
                    st_hist[n]["pv"] = t_pv
                    st_hist.pop(n - 8, None)
                    if pp == 3 and pending:
                        pending.pop()()
                for a in range(2):
                    P.op("dve", "tensor_copy", out=oacc[:, a * 2:a * 2 + 2, :],
                         in_=bank(AB[a], 264).rearrange("p (a b) -> p a b", a=2), waits=[t_pv])
                t_ev = P.last_tok("dve")
                T[("accev", h, qb)] = t_ev
                P.op("dve", "reciprocal", out=rec[:, :], in_=oacc[:, :, 128], dep=True)
                P.op("dve", "tensor_tensor", out=o_tok[:, :, :], in0=oacc[:, :, 0:128], in1=bc_last(rec[:, :], 128),
                     op=ALU.mult, dep=True, waits=[T.get(("otr", h, qb - 1)) if qb > 0 else T.get(("otr", h - 1, 3))])
                t_o = P.last_tok("dve")

                def fin(h=h, qb=qb, q0=q0, t_o=t_o):
                    wv = [t_o, T.get(("oev", h, qb - 1)) if qb > 0 else T.get(("oev", h - 1, 3)),
                          T.get(("qev", h, NQT - 2))]
                    for qt in range(4):
                        t = P.op("pe", "transpose", out=bankb(0, 128, qt * 128), in_=o_tok[:, qt, :],
                                 identity=ident[:, :], waits=wv, sig=(qt == 3))
                    T[("otr", h, qb)] = t
                    P.op("act", "activation", out=oT_mla[:, h, q0:q0 + 512], in_=bankb(0, 512), func=AF.Copy,
                         waits=[t])
                    T[("oev", h, qb)] = P.last_tok("act")
                if qb < 3:
                    pending.append(fin)
                else:
                    fin()
            T[("att_done", h)] = P.last_tok("pe")
            T[("wfree", h)] = T[("att_done", h)]
        stg.finish()
        if debug and mla_att:
            P.op("sp", "dma_start", out=dbg_oTm, in_=oT_mla[:, :, :], waits=[T[("oev", mla_heads - 1, 3)]], dsem=s_done)
            P.op("sp", "wait_ge", s_done.h, s_done.n)
        P.flush()
        A.release(m)

    def phase_DA():
        m = A.mark()
        QdaT = A.alloc("QdaT", [128, 8, NQ], BF16)
        s_qd = P.sem("qdld")
        t_qd = None
        for hq in range(4):
            t_qd = P.op("sp", "dma_start", out=QdaT[:, hq * 2:hq * 2 + 2, :], in_=QTd[:, hq * 2:hq * 2 + 2, :], dsem=s_qd)
        KTh = [A.alloc("KTh", [128, NTK], BF16) for _ in range(2)]
        Vh = [A.alloc("Vh", [128, NKT, 132], BF16) for _ in range(2)]
        NPT = 3
        PT = [A.alloc("dPT", [128, 1024], BF16) for _ in range(NPT)]
        oacc = A.alloc("doacc", [128, 8, 132], F32)
        rec = A.alloc("drec", [128, 8], F32)
        t0_ = A.alloc("dt0", [128, 4, 128], F32)
        t1_ = A.alloc("dt1", [128, 4, 128], F32)
        osq = A.alloc("dosq", [128, 4, 128], F32)
        ssq = A.alloc("dssq", [128, 8], F32)
        o_tok = A.alloc("do_tok", [128, 4, 128], BF16)
        subs = A.alloc("dsubs", [128, 128], F32)
        s_k = [P.sem("dk") for _ in range(2)]
        for c in range(2):
            P.op("dve", "memset", Vh[c][:, :, 128:129], 1.0)
        P.op("dve", "tensor_scalar", out=subs[:, :], in0=subln[:, :], scalar1=1.0 - LAM_INIT, scalar2=None,
             op0=ALU.mult)
        t_ones = P.last_tok("dve")
        Vd3 = Vd.rearrange("(t p) c -> p t c", p=128)
        T = {}
        stg = Stager(chunks_d, 5632)
        pending = []
        SP_ = ((1, 2), (3, 4))
        AB = (5, 6, 7)
        cnt = {"st": 0}
        st_hist = {}
        t_pv = None
        for h in range(8):
            c = h % 2
            P.op("sp", "dma_start", out=KTh[c][:, :], in_=KTd[h, :, :], waits=[T.get(("att_done", h - 2)), t_ones],
                 dsem=s_k[c])
            for part in range(3):
                t_kv = P.op("sp", "dma_start", out=Vh[c][:, part * 22:(part + 1) * 22, 0:128],
                            in_=Vd3[:, part * 22:(part + 1) * 22, h * 128:(h + 1) * 128], dsem=s_k[c])
            for qb in range(4):
                q0 = qb * 512
                prev = T.get("last_accev")

                def emit_S(kt):
                    n = cnt["st"]
                    cnt["st"] += 1
                    p = SP_[n % 2]
                    for u in range(2):
                        t_s = P.op("pe", "matmul", out=bank(p[u]),
                                   lhsT=KTh[c][u * 64:(u + 1) * 64, kt * 128:(kt + 1) * 128],
                                   rhs=QdaT[u * 64:(u + 1) * 64, h, q0:q0 + 512], start=True, stop=True,
                                   waits=[t_kv, t_qd, st_hist.get(n - 2, {}).get("exp")], sig=(u == 1))
                    st_hist[n] = {"s": t_s}
                    return n
                n_next = emit_S(0)
                for kt in range(NKT):
                    n = n_next
                    if kt + 1 < NKT:
                        n_next = emit_S(kt + 1)
                    p = SP_[n % 2]
                    pt = PT[n % NPT]
                    t_e = P.op("act", "activation", out=pt[:, :], in_=ps[:, p[0] * 512:p[0] * 512 + 1024], func=AF.Exp,
                               scale=DA_SCALE, waits=[st_hist[n]["s"], st_hist.get(n - NPT, {}).get("pv")], sig=True)
                    st_hist[n]["exp"] = t_e
                    for u in range(2):
                        for qt in range(4):
                            g = u * 4 + qt
                            t_pv = P.op("pe", "matmul", out=bank(AB[g // 3], 129, (g % 3) * 132),
                                        lhsT=pt[:, u * 512 + qt * 128: u * 512 + (qt + 1) * 128], rhs=Vh[c][:, kt, 0:129],
                                        start=(kt == 0 and g % 3 == 0), stop=(kt == NKT - 1), skip_group_check=True,
                                        waits=[t_e] + ([prev] if kt == 0 else []), sig=(g == 7))
                    st_hist[n]["pv"] = t_pv
                    st_hist.pop(n - 8, None)
                    if kt == 6 and pending:
                        pending.pop()()
                P.op("dve", "tensor_copy", out=oacc[:, 0:3, :], in_=bank(AB[0], 396).rearrange("p (a b) -> p a b", a=3),
                     waits=[t_pv])
                P.op("dve", "tensor_copy", out=oacc[:, 3:6, :], in_=bank(AB[1], 396).rearrange("p (a b) -> p a b", a=3))
                P.op("dve", "tensor_copy", out=oacc[:, 6:8, :], in_=bank(AB[2], 264).rearrange("p (a b) -> p a b", a=2))
                T["last_accev"] = P.last_tok("dve")
                P.op("dve", "reciprocal", out=rec[:, :], in_=oacc[:, :, 128], dep=True)
                P.op("dve", "tensor_scalar", out=rec[:, 4:8], in0=rec[:, 4:8], scalar1=lamt[:, 0:1], scalar2=None,
                     op0=ALU.mult, dep=True)
                P.op("dve", "tensor_tensor", out=t0_[:, :, :], in0=oacc[:, 0:4, 0:128], in1=bc_last(rec[:, 0:4], 128),
                     op=ALU.mult, dep=True)
                P.op("dve", "tensor_tensor", out=t1_[:, :, :], in0=oacc[:, 4:8, 0:128], in1=bc_last(rec[:, 4:8], 128),
                     op=ALU.mult)
                P.op("dve", "tensor_tensor", out=t0_[:, :, :], in0=t0_[:, :, :], in1=t1_[:, :, :], op=ALU.add, dep=True)
                P.op("dve", "tensor_tensor", out=osq[:, :, :], in0=t0_[:, :, :], in1=t0_[:, :, :], op=ALU.mult, dep=True)
                P.op("dve", "tensor_reduce", out=ssq[:, 0:4], in_=osq[:, :, :], axis=AXX, op=ALU.add, dep=True)
                t_ss = P.last_tok("dve")
                t_sq = P.op("act", "activation", out=ssq[:, 4:8], in_=ssq[:, 0:4], func=AF.Sqrt, scale=1.0 / 128,
                            bias=eps_t[:, :], waits=[t_ss], sig=True)
                P.op("dve", "reciprocal", out=ssq[:, 4:8], in_=ssq[:, 4:8], waits=[t_sq])
                P.op("dve", "tensor_tensor", out=osq[:, :, :], in0=t0_[:, :, :], in1=bc_last(ssq[:, 4:8], 128),
                     op=ALU.mult, dep=True)
                P.op("dve", "tensor_tensor", out=o_tok[:, :, :], in0=osq[:, :, :], in1=bc_mid(subs[:, :], 4),
                     op=ALU.mult, dep=True, waits=[T.get("last_otr")])
                t_o = P.last_tok("dve")

                def fin(h=h, q0=q0, t_o=t_o):
                    wv = [t_o, T.get("last_oev")]
                    for qt in range(4):
                        t = P.op("pe", "transpose", out=bankb(0, 128, qt * 128), in_=o_tok[:, qt, :],
                                 identity=ident[:, :], waits=wv, sig=(qt == 3))
                    T["last_otr"] = t
                    P.op("act", "activation", out=oT_da[:, h, q0:q0 + 512], in_=bankb(0, 512), func=AF.Copy, waits=[t])
                    T["last_oev"] = P.last_tok("act")
                pending.append(fin)
            T[("att_done", h)] = t_pv
        while pending:
            pending.pop()()
        stg.finish()
        if debug:
            P.op("sp", "dma_start", out=dbg_oTd, in_=oT_da[:, :, :], waits=[T["last_oev"]], dsem=s_done)
            P.op("sp", "wait_ge", s_done.h, s_done.n)
        P.flush()
        A.release(m)

    wout3 = w_out.rearrange("(k p) n -> p k n", p=128)
    PAIRS = ((2, 3), (4, 5), (6, 7))

    def phase_OF(j):
        t0 = j * 512
        mj = A.mark()
        xres = [A.alloc("xres", [128, D], F32) for _ in range(4)]
        hTb = A.alloc("hTb", [128, KC, 512], BF16)
        mO = A.mark()
        g1_bc = A.alloc("g1bc", [128, D], F32)
        mT = A.alloc("mT", [128, KC, 512], BF16)

        def ab(i):
            return (A1[:, :, 0], B1[:, :, 0]) if i < 4 else (A2[:, :], B2)
        NP = NormPipe("o", (0, 1), 4, lambda i: xq[t0 + i * 128: t0 + (i + 1) * 128, :], ab, NX=4, xt=xres,
                      out_fn=lambda i: hTb[:, :, (i % 4) * 128:(i % 4 + 1) * 128])
        mL = A.mark()
        wg = [A.alloc("wg", [128, 6144], BF16) for _ in range(2)]
        sg = [A.alloc("sg", [128, 512], F32) for _ in range(2)]
        mA_ = [A.alloc("mA", [128, 512], F32) for _ in range(2)]
        tB = A.alloc("tB", [128, 512], F32)
        s_g1 = P.sem("g1")
        s_wg = [P.sem("wg") for _ in range(2)]
        t_g1 = P.op("sp", "dma_start", out=g1_bc[:, :], in_=g_d[:, 0:D], dsem=s_g1)
        for i in range(4):
            NP.load(i)
        NP.stats(0)
        NP.stats(1)
        for i in range(4):
            NP.transposes(i)
            NP.evac(i)
            if i + 2 < 4:
                NP.stats(i + 2)
        t_h1 = [NP.t_hT[i] for i in range(4)]
        bfree = {}
        wg_free = {}
        T = {}
        n_unit = 0
        for fo in range(16):
            c = fo % 2
            t_w = P.op("sp", "dma_start", out=wg[c][:, :], in_=Wg_s[fo, :, :], waits=[wg_free.get(fo - 2)],
                       dsem=s_wg[c])
            for br in range(2):
                pa = PAIRS[n_unit % 3]
                n_unit += 1
                goff = br * 2048
                yoff = 4096 + br * 1024
                oT = oT_da if br == 0 else oT_mla
                wv = [t_h1, t_w, bfree.get(pa[0]), bfree.get(pa[1])]
                for k in range(KC):
                    P.op("pe", "matmul", out=bank(pa[0]), lhsT=wg[c][:, goff + k * 128: goff + (k + 1) * 128],
                         rhs=hTb[:, k, :], start=(k == 0), stop=(k == KC - 1), waits=wv)
                for k in range(8):
                    t = P.op("pe", "matmul", out=bank(pa[1]), lhsT=wg[c][:, yoff + k * 128: yoff + (k + 1) * 128],
                             rhs=oT[:, k, t0:t0 + 512], start=(k == 0), stop=(k == 7), sig=(k == 7))
                if br == 1:
                    wg_free[fo] = t
                u = n_unit % 2
                t_s = P.op("act", "activation", out=sg[u][:, :], in_=bank(pa[0]), func=AF.Sigmoid,
                           waits=[t, T.get(("sgfree", n_unit - 2))], sig=True)
                bfree[pa[0]] = t_s
                if br == 0:
                    a = fo % 2
                    t_m = P.op("dve", "tensor_tensor", out=mA_[a][:, :], in0=bank(pa[1]), in1=sg[u][:, :], op=ALU.mult,
                               waits=[t_s, T.get(("mAfree", fo - 2))], sig=True)
                    T[("sgfree", n_unit)] = t_m
                    bfree[pa[1]] = t_m
                else:
                    t_m = P.op("dve", "tensor_tensor", out=tB[:, :], in0=bank(pa[1]), in1=sg[u][:, :], op=ALU.mult,
                               waits=[t_s, T.get(("tBfree", fo - 1))], sig=True)
                    T[("sgfree", n_unit)] = t_m
                    bfree[pa[1]] = t_m
                    t_p = P.op("pool", "tensor_tensor", out=mT[:, fo, :], in0=mA_[fo % 2][:, :], in1=tB[:, :], op=ALU.add,
                               waits=[t_m], sig=True)
                    T[("mAfree", fo)] = t_p
                    T[("tBfree", fo)] = t_p
        t_mT = T[("mAfree", 15)]
        P.flush()
        A.release(mL)
        wo = [A.alloc("wo", [128, KC, 512], BF16) for _ in range(2)]
        tr_ = [A.alloc("tres", [128, 512], F32) for _ in range(2)]
        s_wo = [P.sem("wo") for _ in range(2)]
        wo_free = {}
        bfree = {}
        T = {}
        n_unit = 0
        RB = (2, 3, 4, 5, 6, 7)
        t_x1 = {}
        for cb in range(4):
            c = cb % 2
            for hf in range(2):
                t_w = P.op("sp", "dma_start", out=wo[c][:, :, hf * 256:(hf + 1) * 256],
                           in_=Wo_s[cb * 2 + hf, :, :].rearrange("p (k c) -> p k c", k=16),
                           waits=[wo_free.get(cb - 2)], dsem=s_wo[c])
            for i in range(4):
                b = RB[n_unit % 6]
                wv = [t_w, bfree.get(b)]
                for k in range(KC):
                    t = P.op("pe", "matmul", out=bank(b), lhsT=mT[:, k, i * 128:(i + 1) * 128], rhs=wo[c][:, k, :],
                             start=(k == 0), stop=(k == KC - 1), waits=wv, sig=(k == KC - 1))
                u = n_unit % 2
                t_d = P.op("dve", "tensor_tensor", out=tr_[u][:, :], in0=bank(b), in1=g1_bc[:, cb * 512:(cb + 1) * 512],
                           op=ALU.mult, waits=[t, t_g1, T.get(("trfree", n_unit - 2))], sig=True)
                bfree[b] = t_d
                t_p = P.op("pool", "tensor_tensor", out=xres[i][:, cb * 512:(cb + 1) * 512],
                           in0=xres[i][:, cb * 512:(cb + 1) * 512], in1=tr_[u][:, :], op=ALU.add,
                           waits=[t_d, NP.t_xn[i]], sig=True)
                T[("trfree", n_unit)] = t_p
                t_x1[i] = t_p
                n_unit += 1
            wo_free[cb] = t
        t_lastmm = t
        for i in range(2):
            NP.stats(4 + i, src=xres[i][:, :], src_wait=t_x1[i])
        for i in range(4):
            ii = 4 + i
            NP.transposes(ii)
            NP.hT_rel[ii - 4] = [t_lastmm]
            NP.evac(ii)
            if i + 2 < 4:
                NP.stats(ii + 2, src=xres[i + 2][:, :], src_wait=t_x1[i + 2])
        P.flush()
        A.release(mO)
        g2_bc = A.alloc("g2bc", [128, D], F32)
        fin_bc = A.alloc("finbc", [128, D], F32)
        actT = A.alloc("actT", [128, 22, 512], BF16)
        wgu = [A.alloc("wgu", [128, 4096], BF16) for _ in range(2)]
        wd = [A.alloc("wd", [128, 5632], BF16) for _ in range(2)]
        sl_ = [A.alloc("silu", [128, 512], F32) for _ in range(2)]
        td_ = [A.alloc("tdn", [128, 256], F32) for _ in range(2)]
        st3 = A.alloc("fst", [128, 8], F32)
        junk3 = A.alloc("fjunk", [128, D], BF16)
        s_g2 = P.sem("g2")
        s_wgu = [P.sem("wgu") for _ in range(2)]
        s_wd = [P.sem("wd") for _ in range(2)]
        s_out = [P.sem("outst") for _ in range(2)]
        P.op("sp", "dma_start", out=g2_bc[:, :], in_=g_d[:, D:2 * D], dsem=s_g2)
        t_g2 = P.op("sp", "dma_start", out=fin_bc[:, :], in_=fin_bc_d, dsem=s_g2)
        bfree = {}
        T = {}
        n_pair = 0
        n_dn = 0
        wgu_free = {}
        wd_free = {}
        n_wgu = 0
        n_wd = 0
        t_acc = {}
        for half in range(2):
            for hc in range(22):
                hcg = half * 22 + hc
                c = n_wgu % 2
                t_w = P.op("sp", "dma_start", out=wgu[c][:, :], in_=Wgu_s[hcg, :, :], waits=[wgu_free.get(n_wgu - 2)],
                           dsem=s_wgu[c])
                pa = PAIRS[n_pair % 3]
                wv = [t_w, bfree.get(pa[0]), bfree.get(pa[1]), T.get("actT_free") if hc == 0 else None]
                for k in range(KC):
                    P.op("pe", "matmul", out=bank(pa[0]), lhsT=wgu[c][:, k * 128:(k + 1) * 128], rhs=hTb[:, k, :],
                         start=(k == 0), stop=(k == KC - 1), waits=wv)
                for k in range(KC):
                    t = P.op("pe", "matmul", out=bank(pa[1]), lhsT=wgu[c][:, 2048 + k * 128:2048 + (k + 1) * 128],
                             rhs=hTb[:, k, :], start=(k == 0), stop=(k == KC - 1), sig=(k == KC - 1))
                wgu_free[n_wgu] = t
                n_wgu += 1
                u = n_pair % 2
                t_s = P.op("act", "activation", out=sl_[u][:, :], in_=bank(pa[0]), func=AF.Silu,
                           waits=[t, T.get(("slfree", n_pair - 2))], sig=True)
                bfree[pa[0]] = t_s
                t_m = P.op("dve", "tensor_tensor", out=actT[:, hc, :], in0=bank(pa[1]), in1=sl_[u][:, :], op=ALU.mult,
                           waits=[t_s], sig=True)
                bfree[pa[1]] = t_m
                T[("slfree", n_pair)] = t_m
                n_pair += 1
            t_act = t_m
            for cb in range(8):
                c = n_wd % 2
                t_w = P.op("sp", "dma_start", out=wd[c][:, :], in_=Wd_s[half * 8 + cb, :, :], waits=[wd_free.get(n_wd - 2)],
                           dsem=s_wd[c])
                for i in range(4):
                    b = RB[n_dn % 6]
                    wv = [t_w, t_act, bfree.get(b)]
                    for k in range(22):
                        t = P.op("pe", "matmul", out=bank(b, 256), lhsT=actT[:, k, i * 128:(i + 1) * 128],
                                 rhs=wd[c][:, k * 256:(k + 1) * 256], start=(k == 0), stop=(k == 21), waits=wv,
                                 sig=(k == 21))
                    u = n_dn % 2
                    t_d = P.op("dve", "tensor_tensor", out=td_[u][:, :], in0=bank(b, 256),
                               in1=g2_bc[:, cb * 256:(cb + 1) * 256], op=ALU.mult,
                               waits=[t, t_g2, T.get(("tdfree", n_dn - 2))], sig=True)
                    bfree[b] = t_d
                    t_p = P.op("pool", "tensor_tensor", out=xres[i][:, cb * 256:(cb + 1) * 256],
                               in0=xres[i][:, cb * 256:(cb + 1) * 256], in1=td_[u][:, :], op=ALU.add, waits=[t_d],
                               sig=True)
                    T[("tdfree", n_dn)] = t_p
                    t_acc[i] = t_p
                    n_dn += 1
                wd_free[n_wd] = t
                n_wd += 1
            T["actT_free"] = t
        for i in range(4):
            c = i % 2
            P.op("act", "activation", out=junk3[:, :], in_=xres[i][:, :], func=AF.Square, accum_out=st3[:, c:c + 1],
                 waits=[t_acc[i]])
            t_sq = P.op("act", "activation", out=st3[:, 2 + c:3 + c], in_=st3[:, c:c + 1], func=AF.Sqrt,
                        scale=1.0 / D, bias=eps_t[:, :], dep=True, sig=True)
            P.op("dve", "reciprocal", out=st3[:, 4 + c:5 + c], in_=st3[:, 2 + c:3 + c], waits=[t_sq, t_acc[i]])
            t_f = P.op("dve", "scalar_tensor_tensor", out=xres[i][:, :], in0=xres[i][:, :], scalar=st3[:, 4 + c:5 + c],
                       in1=fin_bc[:, :], op0=ALU.mult, op1=ALU.mult, dep=True, sig=True)
            P.op("sp", "dma_start", out=out_d[t0 + i * 128:t0 + (i + 1) * 128, :], in_=xres[i][:, :], waits=[t_f],
                 dsem=s_out[c])
        for s_ in s_out:
            P.op("sp", "wait_ge", s_.h, s_.n)
        P.flush()
        A.release(mj)

    if "A" not in skip:
        phase_A()
    if stop_after == "A":
        return nc, dbg
    if "K" not in skip:
        phase_K()
    if stop_after == "K":
        return nc, dbg
    cqnT = A.alloc("cqnT", [128, 4, NQ], BF16, top=True)
    if "Q" not in skip:
        phase_Q()
    if stop_after == "Q":
        return nc, dbg
    oT_mla = A.alloc("oT_mla", [128, 8, NQ], BF16)
    phase_MLA()
    A.hi = SB_END
    if stop_after == "MLA":
        return nc, dbg
    oT_da = A.alloc("oT_da", [128, 8, NQ], BF16)
    phase_DA()
    if stop_after == "DA":
        return nc, dbg
    for j in range(4):
        phase_OF(j)
    return nc, dbg


def _rope_tables(pos_row, pos_col):
    nf = 16
    inv = (10000.0 ** (-np.arange(nf, dtype=np.float32) / nf)).astype(np.float32)
    ar = pos_row[:, None].astype(np.float32) * inv
    ac = pos_col[:, None].astype(np.float32) * inv
    cr, sr, cc, sc = np.cos(ar), np.sin(ar), np.cos(ac), np.sin(ac)
    C64 = np.concatenate([cr, cr, cc, cc], 1)
    S64 = np.concatenate([-sr, sr, -sc, sc], 1)
    return np.concatenate([C64, S64], 1).astype(np.float32)


def make_in_maps(inputs):
    f = lambda a: np.ascontiguousarray(np.asarray(a, dtype=np.float32))
    x = f(inputs["x"]); c = f(inputs["c"]); ctx = f(inputs["ctx"]); c_ctx = f(inputs["c_ctx"])
    t = np.arange(8192)
    ropeL = _rope_tables(t // 64, t % 64)
    ropeC = np.concatenate([np.ones((256, 64), np.float32), np.zeros((256, 64), np.float32)], 1)
    ropeK = np.concatenate([ropeC, ropeL], 0)
    b_ada = f(inputs["b_ada"])[0]
    shared = {
        "w_ada": f(inputs["w_ada"])[0],
        "b_adaT": np.ascontiguousarray(b_ada.reshape(96, 128).T),
        "b_ada_g": np.ascontiguousarray(np.broadcast_to(
            np.concatenate([b_ada[2 * D:3 * D], b_ada[5 * D:6 * D]])[None, :], (128, 2 * D))),
        "n1gT": np.ascontiguousarray(f(inputs["norm1_g"])[0].reshape(KC, 128).T),
        "n2gT": np.ascontiguousarray(f(inputs["norm2_g"])[0].reshape(KC, 128).T),
        "fin_bc": np.ascontiguousarray(np.broadcast_to(f(inputs["final_norm_g"])[None, :], (128, D))),
        "w_in": f(inputs["w_in"])[0],
        "lam_in": np.ascontiguousarray(np.broadcast_to(f(inputs["da_lambda"])[0].reshape(1, 256), (128, 256))),
        "subln_bc": np.ascontiguousarray(np.broadcast_to(f(inputs["da_subln_g"])[0][None, :], (128, 128))),
        "qg_bc": np.ascontiguousarray(np.broadcast_to(f(inputs["mla_q_norm_g"])[0][None, :], (128, 512))),
        "kvg_bc": np.ascontiguousarray(np.broadcast_to(f(inputs["mla_kv_norm_g"])[0][None, :], (128, 256))),
        "w_uq": f(inputs["w_uq"])[0], "w_ukv": f(inputs["w_ukv"])[0],
        "w_o_da": f(inputs["w_o_da"])[0], "w_o_mla": f(inputs["w_o_mla"])[0], "w_out": f(inputs["w_out"])[0],
        "w_gate": f(inputs["w_ffn_gate"])[0], "w_up": f(inputs["w_ffn_up"])[0], "w_down": f(inputs["w_ffn_down"])[0],
        "ropeK": ropeK, "ident": np.eye(128, dtype=np.float32),
    }
    maps = []
    for core in range(8):
        b, qs = core // 4, core % 4
        m = dict(shared)
        m["xk"] = np.ascontiguousarray(np.concatenate([ctx[b], x[b]], 0))
        m["xq"] = np.ascontiguousarray(x[b, qs * NQ:(qs + 1) * NQ])
        cT = np.stack([c[b].reshape(KC, 128).T, c_ctx.reshape(KC, 128).T], -1)
        m["cT"] = np.ascontiguousarray(cT)
        m["ropeQ"] = np.ascontiguousarray(ropeL[qs * NQ:(qs + 1) * NQ])
        maps.append(m)
    return maps


def kernel(**inputs):
    nc, _ = build_nc()
    maps = make_in_maps(inputs)
    res = run_bass_kernel_spmd(nc, maps, core_ids=list(range(8)))
    out = np.zeros((2, 8192, D), np.float32)
    for core in range(8):
        b, qs = core // 4, core % 4
        out[b, qs * NQ:(qs + 1) * NQ] = res.results[core]["out"]
    return out
```

```python
import numpy as np
import concourse.bass as bass
import concourse.mybir as mybir
from concourse.bass_utils import run_bass_kernel_spmd

F32 = mybir.dt.float32
BF16 = mybir.dt.bfloat16
AF = mybir.ActivationFunctionType
ALU = mybir.AluOpType
AXX = mybir.AxisListType.X

D = 2048
KC = 16
NTK = 8448
NKT = 66
NQ = 2048
NQT = 16
FH = 5632
FHC = 44
EPS = 1e-6
LAM_INIT = 0.8 - 0.6 * 1.0
SB_BASE = 16512
SB_END = 229312
DA_SCALE = 64 ** -0.5
MLA_SCALE = 192 ** -0.5


class Sem:
    _all = []

    def __init__(self, nc, name):
        self.h = nc.alloc_semaphore(name=name)
        self.n = 0
        self.uid = len(Sem._all)
        Sem._all.append(self)


class Prog:
    ENG = ("pe", "act", "dve", "pool", "sp")

    def __init__(self, nc):
        self.nc = nc
        self.q = {e: [] for e in self.ENG}
        self.seen = {e: {} for e in self.ENG}
        self.nsem = 0
        self.prog = {e: Sem(nc, "prog_" + e) for e in ("pe", "act", "dve", "pool")}

    def sem(self, name, sw=False):
        if not hasattr(self, "pool_free"):
            self.pool_free = {False: [], True: []}
            self.pool_used = {False: [], True: []}
        if self.pool_free[sw]:
            s = self.pool_free[sw].pop()
        else:
            self.nsem += 1
            s = Sem(self.nc, "dsem%s_%d" % ("sw" if sw else "hw", self.nsem))
        self.pool_used[sw].append(s)
        return s

    def op(self, eng, meth, *args, waits=(), sig=False, dsem=None, dep=False, **kw):
        q = self.q[eng]
        w = []
        if dep and q:
            prev = q[-1]
            if prev[4] is None:
                s = self.prog[eng]
                s.n += 1
                prev[4] = (s, 1, s.n)
            w.append((prev[4][0], prev[4][2]))

        def add(t):
            if t is None:
                return
            if isinstance(t, list):
                for u in t:
                    add(u)
                return
            w.append(t)
        for t in waits:
            add(t)
        ww = []
        seen = self.seen[eng]
        for (s, v) in w:
            if v <= 0 or seen.get(s.uid, 0) >= v:
                continue
            seen[s.uid] = v
            ww.append((s, v))
        inc = None
        tok = None
        if dsem is not None:
            dsem.n += 16
            inc = (dsem, 16, dsem.n)
            tok = (dsem, dsem.n)
        elif sig:
            s = self.prog[eng]
            s.n += 1
            inc = (s, 1, s.n)
            tok = (s, s.n)
        q.append([meth, args, kw, ww, inc])
        return tok

    def last_tok(self, eng):
        prev = self.q[eng][-1]
        if prev[4] is None:
            s = self.prog[eng]
            s.n += 1
            prev[4] = (s, 1, s.n)
        return (prev[4][0], prev[4][2])

    def simulate(self):
        if not hasattr(self, "simcnt"):
            self.simcnt = {}
        cnt = self.simcnt
        pos = {e: 0 for e in self.ENG}
        progress = True
        while progress:
            progress = False
            for e in self.ENG:
                q = self.q[e]
                while pos[e] < len(q):
                    meth, args, kw, waits, inc = q[pos[e]]
                    ok = all(cnt.get(s.uid, 0) >= v for (s, v) in waits)
                    if not ok:
                        break
                    if inc is not None:
                        cnt[inc[0].uid] = cnt.get(inc[0].uid, 0) + inc[1]
                    pos[e] += 1
                    progress = True
        stuck = {e: pos[e] for e in self.ENG if pos[e] < len(self.q[e])}
        if stuck:
            msg = []
            for e, p in stuck.items():
                meth, args, kw, waits, inc = self.q[e][p]
                bad = [(s.h, v, cnt.get(s.uid, 0)) for (s, v) in waits if cnt.get(s.uid, 0) < v]
                msg.append("%s stuck at op %d/%d %s out=%s unsatisfied=%s" % (e, p, len(self.q[e]), meth,
                           str(kw.get("out", ""))[:80], bad))
            raise RuntimeError("DEADLOCK in sync plan:\n" + "\n".join(msg))

    def flush(self):
        nc = self.nc
        self.simulate()
        if hasattr(self, "pool_used"):
            for k in (False, True):
                self.pool_free[k].extend(self.pool_used[k])
                self.pool_used[k] = []
        import os as _os
        if _os.environ.get("KSTAT"):
            print("PHASE stats:", {e: (len(self.q[e]), sum(len(o[3]) for o in self.q[e])) for e in self.ENG},
                  "nsem", len(Sem._all), "sem ids", [str(x.h)[-12:] for x in Sem._all[-3:]])
        q = self.q
        self.q = {e: [] for e in self.ENG}
        with nc.Block() as blk:
            def mk(key):
                def body(e):
                    for meth, args, kw, waits, inc in q[key]:
                        for (s, v) in waits:
                            e.wait_ge(s.h, v)
                        ins = getattr(e, meth)(*args, **kw)
                        if inc is not None:
                            ins.then_inc(inc[0].h, inc[1])
                return body
            for key, reg in (("pe", blk.tensor), ("act", blk.scalar), ("dve", blk.vector),
                             ("pool", blk.gpsimd), ("sp", blk.sync)):
                if q[key]:
                    reg(mk(key))


class Arena:
    def __init__(self, nc):
        self.nc = nc
        self.lo = SB_BASE
        self.hi = SB_END
        self.cnt = 0

    def alloc(self, name, shape, dtype, top=False):
        n = 1
        for s in shape[1:]:
            n *= s
        sz = (n * (2 if dtype == BF16 else 4) + 63) // 64 * 64
        if top:
            self.hi -= sz
            off = self.hi
        else:
            off = self.lo
            self.lo += sz
        assert self.lo <= self.hi, "SBUF overflow at %s: lo=%d hi=%d" % (name, self.lo, self.hi)
        self.cnt += 1
        return self.nc.alloc_sbuf_tensor_at("%s_%d" % (name, self.cnt), list(shape), dtype, offset=off)

    def mark(self):
        return (self.lo, self.hi)

    def release(self, m):
        self.lo, self.hi = m


def bc_mid(ap, n):
    return ap.unsqueeze(1).broadcast_to([ap.shape[0], n] + list(ap.shape[1:]))


def bc_last(ap, n):
    return ap.unsqueeze(len(ap.shape)).broadcast_to(list(ap.shape) + [n])


class Ctx:
    pass


def build_nc(debug=False, stop_after=None, mla_heads=8, mla_att=True, mla_stage=3, mla_q=9, skip=()):
    nc = bass.Bass("TRN2", target_bir_lowering=False)
    Sem._all = []
    C = Ctx()
    C.nc = nc

    def din(name, shape, dt=F32):
        return nc.dram_tensor(name, list(shape), dt, kind="ExternalInput").ap()

    xk = din("xk", [NTK, D])
    xq = din("xq", [NQ, D])
    cT_d = din("cT", [128, KC, 2])
    w_ada = din("w_ada", [D, 6 * D])
    b_adaT_d = din("b_adaT", [128, 96])
    b_ada_g = din("b_ada_g", [128, 2 * D])
    n1gT_d = din("n1gT", [128, KC])
    n2gT_d = din("n2gT", [128, KC])
    fin_bc_d = din("fin_bc", [128, D])
    w_in = din("w_in", [D, 8000])
    lam_d = din("lam_in", [128, 256])
    subln_d = din("subln_bc", [128, 128])
    qg_d = din("qg_bc", [128, 512])
    kvg_d = din("kvg_bc", [128, 256])
    w_uq = din("w_uq", [512, 1536])
    w_ukv = din("w_ukv", [256, 2048])
    w_o_da = din("w_o_da", [1024, D])
    w_o_mla = din("w_o_mla", [1024, D])
    w_out = din("w_out", [D, D])
    w_gate = din("w_gate", [D, FH])
    w_up = din("w_up", [D, FH])
    w_down = din("w_down", [FH, D])
    ropeK = din("ropeK", [NTK, 128])
    ropeQ = din("ropeQ", [NQ, 128])
    ident_d = din("ident", [128, 128])
    out_d = nc.dram_tensor("out", [NQ, D], F32, kind="ExternalOutput").ap()

    dbg = {}

    def dscr(name, shape, dt=F32, force_out=False):
        kind = "ExternalOutput" if (debug or force_out) else "Internal"
        t = nc.dram_tensor(name, list(shape), dt, kind=kind).ap()
        dbg[name] = t
        return t

    KTd = dscr("KTd", [8, 128, NTK], BF16)
    Vd = dscr("Vd", [NTK, 1024], BF16)
    g_d = dscr("g_rows", [128, 2 * D], F32)
    if debug:
        dbg_modT = dscr("dbg_modT", [128, 96, 2])
        dbg_ckvnT = dscr("dbg_ckvnT", [128, 2, NTK], BF16)
        dbg_krT = dscr("dbg_krT", [128, NTK], BF16)
        dbg_cqnT = dscr("dbg_cqnT", [128, 4, NQ], BF16)
        dbg_knT = dscr("dbg_knT", [128, NTK], BF16)
        dbg_Vm = dscr("dbg_Vm", [128, NKT, 132], BF16)
        dbg_qnT = dscr("dbg_qnT", [128, NQ], BF16)
        dbg_qrT = dscr("dbg_qrT", [128, NQ], BF16)
        dbg_oTm = dscr("dbg_oTm", [128, 8, NQ], BF16)
        dbg_oTd = dscr("dbg_oTd", [128, 8, NQ], BF16)

    P = Prog(nc)
    import os as _os
    for _i in range(int(_os.environ.get("DUMMYSEM", "0"))):
        P.sem("dummy")
    A = Arena(nc)
    ps = nc.alloc_psum_tensor("ps", [128, 4096], F32)
    psb = ps[:, :].bitcast(BF16)

    def bank(b, n=512, off=0):
        return ps[:, b * 512 + off: b * 512 + off + n]

    def bankb(b, n=1024, off=0):
        return psb[:, b * 1024 + off: b * 1024 + off + n]

    s_done = P.sem("done")
    done_toks = []

    ident = A.alloc("ident", [128, 128], BF16)
    modT = A.alloc("modT", [128, 96, 2], F32)
    A1 = A.alloc("A1", [128, KC, 2], F32)
    A2 = A.alloc("A2", [128, KC], F32)
    n1gT = A.alloc("n1gT", [128, KC], F32)
    n2gT = A.alloc("n2gT", [128, KC], F32)
    eps_t = A.alloc("eps", [128, 1], F32)
    lamt = A.alloc("lamt", [128, 8], F32)
    subln = A.alloc("subln", [128, 128], F32)
    qg_bc = A.alloc("qg", [128, 512], F32)
    kvg_bc = A.alloc("kvg", [128, 256], F32)
    B1 = modT[:, 0:16, :]
    B2 = modT[:, 48:64, 0]

    def phase_A():
        m = A.mark()
        cT = A.alloc("cT", [128, KC, 2], F32)
        s2f = A.alloc("s2f", [128, KC, 2], F32)
        s2b = A.alloc("s2b", [128, KC, 2], BF16)
        srep = A.alloc("srep", [128, KC, 128], BF16)
        b_adaT = A.alloc("b_adaT", [128, 96], F32)
        lam_in = A.alloc("lam_in", [128, 256], F32)
        lam_pr = A.alloc("lam_pr", [128, 256], F32)
        NSL = 3
        slabs = [A.alloc("adaslab", [128, KC, 512], BF16) for _ in range(NSL)]
        gtmp = [A.alloc("gtmp", [128, 512], F32) for _ in range(2)]
        gbias = [A.alloc("gbias", [128, 512], F32) for _ in range(2)]
        s_c = P.sem("constld")
        s_ci = P.sem("identld", sw=True)
        t_c = None
        t_ci = P.op("pool", "dma_start", out=ident[:, :], in_=ident_d, dsem=s_ci)
        for (dst, src, eng) in [(cT, cT_d, "sp"), (b_adaT, b_adaT_d, "sp"),
                                (n1gT, n1gT_d, "sp"), (n2gT, n2gT_d, "sp"), (lam_in, lam_d, "sp"),
                                (subln, subln_d, "sp"), (qg_bc, qg_d, "sp"), (kvg_bc, kvg_d, "sp")]:
            t_c = P.op(eng, "dma_start", out=dst[tuple(slice(None) for _ in dst.shape)], in_=src, dsem=s_c)
        P.op("dve", "memset", eps_t[:, :], EPS, waits=[t_c, t_ci])
        t_silu = P.op("act", "activation", out=s2f[:, :, :], in_=cT[:, :, :], func=AF.Silu, waits=[t_c], sig=True)
        P.op("dve", "tensor_copy", out=s2b[:, :, :], in_=s2f[:, :, :], waits=[t_silu])
        t_srep = P.op("dve", "tensor_copy", out=srep[:, :, :], in_=bc_last(s2f[:, :, 0], 128), sig=True)
        P.op("dve", "tensor_tensor", out=lam_pr[:, 0:64], in0=lam_in[:, 0:64], in1=lam_in[:, 64:128], op=ALU.mult)
        P.op("dve", "tensor_tensor", out=lam_pr[:, 64:128], in0=lam_in[:, 128:192], in1=lam_in[:, 192:256],
             op=ALU.mult)
        t_lp = P.op("dve", "tensor_reduce", out=lamt[:, 1:3],
                    in_=lam_pr[:, 0:128].rearrange("p (a b) -> p a b", a=2), axis=AXX, op=ALU.add, dep=True, sig=True)
        t_le = P.op("act", "activation", out=lamt[:, 3:5], in_=lamt[:, 1:3], func=AF.Exp, waits=[t_lp], sig=True)
        P.op("dve", "tensor_tensor", out=lamt[:, 5:6], in0=lamt[:, 4:5], in1=lamt[:, 3:4], op=ALU.subtract,
             waits=[t_le])
        P.op("dve", "tensor_scalar", out=lamt[:, 0:1], in0=lamt[:, 5:6], scalar1=-LAM_INIT, scalar2=None,
             op0=ALU.add, dep=True)

        s_slab = [P.sem("slab", sw=True) for _ in range(NSL)]
        s_gb = [P.sem("gb") for _ in range(2)]
        s_gst = [P.sem("gst") for _ in range(2)]
        adaview = w_ada.rearrange("(k p) n -> p k n", p=128)
        slab_free = []
        g_hist = []
        gi = 0
        for cb in range(24):
            sl = cb % NSL
            t_ld = P.op("pool", "dma_start", out=slabs[sl][:, :, :], in_=adaview[:, :, cb * 512:(cb + 1) * 512],
                        waits=[slab_free[cb - NSL]] if cb >= NSL else [], dsem=s_slab[sl])
            isg = cb in (8, 9, 10, 11, 20, 21, 22, 23)
            t_last = None
            for c4 in range(4):
                ch = cb * 4 + c4
                for k in range(KC):
                    last = (c4 == 3 and k == KC - 1)
                    t_last = P.op("pe", "matmul", out=bank(0, 2, ch * 2),
                                  lhsT=slabs[sl][:, k, c4 * 128:(c4 + 1) * 128], rhs=s2b[:, k, :],
                                  start=(k == 0), stop=(k == KC - 1), waits=[t_ld, t_srep], sig=last)
            if isg:
                gb_ = 1 + gi % 2
                wv = [g_hist[gi - 2][0]] if gi >= 2 else []
                for k in range(KC):
                    t_last = P.op("pe", "matmul", out=bank(gb_), lhsT=srep[:, k, :], rhs=slabs[sl][:, k, :],
                                  start=(k == 0), stop=(k == KC - 1), waits=wv, sig=(k == KC - 1))
                gcol = (cb - 8) * 512 if cb < 12 else D + (cb - 20) * 512
                t_b = P.op("sp", "dma_start", out=gbias[gi % 2][:, :], in_=b_ada_g[:, gcol:gcol + 512],
                           waits=wv, dsem=s_gb[gi % 2])
                t_d = P.op("dve", "tensor_tensor", out=gtmp[gi % 2][:, :], in0=bank(gb_), in1=gbias[gi % 2][:, :],
                           op=ALU.add, waits=[t_last, t_b] + ([g_hist[gi - 2][1]] if gi >= 2 else []), sig=True)
                t_s = P.op("sp", "dma_start", out=g_d[:, gcol:gcol + 512], in_=gtmp[gi % 2][:, :], waits=[t_d],
                           dsem=s_gst[gi % 2])
                g_hist.append((t_d, t_s))
                gi += 1
            slab_free.append(t_last)
        P.op("dve", "tensor_tensor", out=modT[:, :, :], in0=bank(0, 192).rearrange("p (a b) -> p a b", b=2),
             in1=bc_last(b_adaT[:, :], 2), op=ALU.add, waits=[slab_free[-1]])
        P.op("dve", "tensor_scalar", out=A1[:, :, :], in0=modT[:, 16:32, :], scalar1=1.0, scalar2=None, op0=ALU.add,
             dep=True)
        P.op("dve", "tensor_tensor", out=A1[:, :, :], in0=A1[:, :, :], in1=bc_last(n1gT[:, :], 2), op=ALU.mult,
             dep=True)
        P.op("dve", "tensor_scalar", out=A2[:, :], in0=modT[:, 64:80, 0], scalar1=1.0, scalar2=None, op0=ALU.add)
        t_A = P.op("dve", "tensor_tensor", out=A2[:, :], in0=A2[:, :], in1=n2gT[:, :], op=ALU.mult, dep=True,
                   sig=True)
        for h in g_hist[-2:]:
            P.op("sp", "wait_ge", h[1][0].h, h[1][1])
        if debug:
            done_toks.append(P.op("sp", "dma_start", out=dbg_modT, in_=modT[:, :, :], waits=[t_A], dsem=s_done))
            P.op("sp", "wait_ge", s_done.h, s_done.n)
        P.flush()
        A.release(m)

    class NormPipe:
        def __init__(self, name, tb, nh, src_fn, ab_fn, NX=2, xt=None, out_fn=None):
            self.tb = tb
            self.NX = NX
            self.nh = nh
            self.src_fn = src_fn
            self.ab_fn = ab_fn
            self.out_fn = out_fn
            self.xt = xt if xt is not None else [A.alloc(name + "xt", [128, D], F32) for _ in range(NX)]
            self.xn = [A.alloc(name + "xn", [128, D], BF16) for _ in range(2)]
            self.tmp = [A.alloc(name + "tmp", [128, 8, 128], F32) for _ in range(2)]
            self.hT = [A.alloc(name + "hT", [128, KC, 128], BF16) for _ in range(nh)] if out_fn is None else None
            self.st = A.alloc(name + "st", [128, 8], F32)
            self.s_x = [P.sem(name + "x") for _ in range(NX)]
            self.t_load = {}
            self.t_xfree = {}
            self.t_xn = {}
            self.t_tr = {}
            self.t_mul = {}
            self.t_add = {}
            self.t_hT = {}
            self.hT_rel = {}
            self.t_trdone = {}

        def load(self, i, extra_waits=()):
            sl = i % self.NX
            self.t_load[i] = P.op("sp", "dma_start", out=self.xt[sl][:, :], in_=self.src_fn(i),
                                  waits=[self.t_xfree.get(i - self.NX)] + list(extra_waits), dsem=self.s_x[sl])

        def stats(self, i, src=None, src_wait=None):
            x = self.xt[i % self.NX][:, :] if src is None else src
            w0 = [self.t_load[i]] if src is None else [src_wait]
            c = i % 2
            st = self.st
            P.op("act", "activation", out=self.xn[c][:, :], in_=x, func=AF.Square, accum_out=st[:, c:c + 1],
                 waits=w0 + [self.t_xn.get(i - 2), self.t_trdone.get(i - 2)])
            t_sq = P.op("act", "activation", out=st[:, 2 + c:3 + c], in_=st[:, c:c + 1], func=AF.Sqrt,
                        scale=1.0 / D, bias=eps_t[:, :], dep=True, sig=True)
            P.op("dve", "reciprocal", out=st[:, 4 + c:5 + c], in_=st[:, 2 + c:3 + c], waits=[t_sq])
            self.t_xn[i] = P.op("dve", "tensor_scalar", out=self.xn[c][:, :], in0=x, scalar1=st[:, 4 + c:5 + c],
                                scalar2=None, op0=ALU.mult, dep=True, waits=w0 + [self.t_trdone.get(i - 2)],
                                sig=True)
            self.t_xfree[i] = [self.t_xn[i]]

        def transposes(self, i):
            c = i % 2
            for h in range(2):
                wv = [self.t_xn[i], self.t_mul.get((i - 1, h))]
                for k in range(8):
                    kk = h * 8 + k
                    t = P.op("pe", "transpose", out=bankb(self.tb[h], 128, k * 128),
                             in_=self.xn[c][:, kk * 128:(kk + 1) * 128], identity=ident[:, :], waits=wv,
                             sig=(k == 7))
                self.t_tr[(i, h)] = t
            self.t_trdone[i] = t

        def evac(self, i):
            Aap, Bap = self.ab_fn(i)
            sl = i % self.nh
            dst = self.hT[sl] if self.out_fn is None else self.out_fn(i)
            for h in range(2):
                self.t_mul[(i, h)] = P.op(
                    "dve", "tensor_tensor", out=self.tmp[h][:, :, :],
                    in0=bankb(self.tb[h]).rearrange("p (a b) -> p a b", a=8),
                    in1=bc_last(Aap[:, h * 8:(h + 1) * 8], 128), op=ALU.mult,
                    waits=[self.t_tr[(i, h)], self.t_add.get((i - 1, h))], sig=True)
                self.t_add[(i, h)] = P.op(
                    "pool", "tensor_tensor", out=dst[:, h * 8:(h + 1) * 8, :], in0=self.tmp[h][:, :, :],
                    in1=bc_last(Bap[:, h * 8:(h + 1) * 8], 128), op=ALU.add,
                    waits=[self.t_mul[(i, h)], self.hT_rel.get(i - self.nh)], sig=True)
            self.t_hT[i] = self.t_add[(i, 1)]

    def rope_ops(src3, nsub, cs, tC, tS, dst2, waits):
        n = nsub * 64
        P.op("dve", "tensor_tensor", out=tC[:, 0:n].rearrange("p (s d) -> p s d", s=nsub), in0=src3,
             in1=bc_mid(cs[:, 0:64], nsub), op=ALU.mult, waits=waits)
        src5 = src3.rearrange("p s (h x j) -> p s h x j", h=2, x=2)
        tS5 = tS[:, 0:n].rearrange("p (s h x j) -> p s h x j", s=nsub, h=2, x=2)
        S4 = cs[:, 64:128].rearrange("p (h x j) -> p h x j", h=2, x=2)
        P.op("dve", "tensor_tensor", out=tS5[:, :, :, 0, :], in0=src5[:, :, :, 1, :],
             in1=bc_mid(S4[:, :, 0, :], nsub), op=ALU.mult)
        t_rd = P.op("dve", "tensor_tensor", out=tS5[:, :, :, 1, :], in0=src5[:, :, :, 0, :],
                    in1=bc_mid(S4[:, :, 1, :], nsub), op=ALU.mult, sig=True)
        return t_rd

    mK = A.mark()
    ckvnT = A.alloc("ckvnT", [128, 2, NTK], BF16, top=True)
    krT2 = A.alloc("krT2", [128, NTK], BF16, top=True)

    def phase_K():
        m = A.mark()
        wkv = A.alloc("wkv", [128, KC, 2368], BF16)
        s_w = P.sem("wkv", sw=True)
        w3 = w_in.rearrange("(k p) n -> p k n", p=128)
        t_w = None
        for kq in range(4):
            P.op("pool", "dma_start", out=wkv[:, kq * 4:(kq + 1) * 4, 0:2048], in_=w3[:, kq * 4:(kq + 1) * 4, 1024:3072],
                 dsem=s_w)
            t_w = P.op("pool", "dma_start", out=wkv[:, kq * 4:(kq + 1) * 4, 2048:2368],
                       in_=w3[:, kq * 4:(kq + 1) * 4, 3584:3904], dsem=s_w)

        def ab(i):
            s = 1 if i < 2 else 0
            return A1[:, :, s], B1[:, :, s]
        NP = NormPipe("k", (0, 1), 2, lambda i: xk[i * 128:(i + 1) * 128, :], ab, NX=3)
        cs = [A.alloc("kcs", [128, 128], F32) for _ in range(2)]
        s_cs = [P.sem("kcs") for _ in range(2)]
        tC = A.alloc("ktC", [128, 1024], F32)
        tS = A.alloc("ktS", [128, 1024], F32)
        tC2 = A.alloc("ktC2", [128, 64], F32)
        tS2 = A.alloc("ktS2", [128, 64], F32)
        krot = [A.alloc("krot", [128, 1024], BF16) for _ in range(2)]
        krr = [A.alloc("krr", [128, 128], BF16) for _ in range(2)]
        ckvn = [A.alloc("ckvn", [128, 256], BF16) for _ in range(2)]
        kst = [A.alloc("kst", [128, 8, 256], BF16) for _ in range(2)]
        vst = [A.alloc("vst", [128, 1024], BF16) for _ in range(2)]
        st2 = A.alloc("kst2", [128, 8], F32)
        junk2 = A.alloc("kjunk2", [128, 256], BF16)
        s_kst = [P.sem("kstd") for _ in range(2)]
        s_vst = [P.sem("vstd") for _ in range(2)]
        KTv = KTd.rearrange("h p t -> p h t")
        T = {}

        def proj_U1(i):
            hT = NP.hT[i % 2]
            wv = [NP.t_hT[i], t_w, T.get(("rope_rd", i - 1)), T.get(("vcopy", i - 1))]
            for k in range(KC):
                for j in range(4):
                    t = P.op("pe", "matmul", out=bank(2 + j), lhsT=hT[:, k, :], rhs=wkv[:, k, j * 512:(j + 1) * 512],
                             start=(k == 0), stop=(k == KC - 1), waits=wv, sig=(k == KC - 1 and j == 3))
            T[("U1", i)] = t

        def proj_U2(i):
            hT = NP.hT[i % 2]
            wv = [T.get(("smallev", i - 1)), T.get(("ckvn", i - 1)), T.get(("kr_rd", i - 1)), T.get(("sq2", i - 1))]
            for k in range(KC):
                t = P.op("pe", "matmul", out=bank(6, 320), lhsT=hT[:, k, :], rhs=wkv[:, k, 2048:2368],
                         start=(k == 0), stop=(k == KC - 1), waits=wv, sig=(k == KC - 1))
            T[("U2", i)] = t
            NP.hT_rel[i] = [t]

        def post(i):
            c = i % 2
            t_cs = P.op("sp", "dma_start", out=cs[c][:, :], in_=ropeK[i * 128:(i + 1) * 128, :],
                        waits=[T.get(("rope_rd", i - 2)), T.get(("kr_rd", i - 2))], dsem=s_cs[c])
            t_rd = rope_ops(bank(2, 1024).rearrange("p (s d) -> p s d", s=16), 16, cs[c], tC, tS, krot[c],
                            [T[("U1", i)], t_cs, T.get(("krot_add", i - 1))])
            T[("rope_rd", i)] = t_rd
            T[("krot_add", i)] = P.op("pool", "tensor_tensor", out=krot[c][:, :], in0=tC[:, :], in1=tS[:, :],
                                      op=ALU.add, waits=[t_rd, T.get(("trK", i - 2))], sig=True)
            T[("vcopy", i)] = P.op("act", "activation", out=vst[c][:, :], in_=bank(4, 1024), func=AF.Copy,
                                   waits=[T[("U1", i)], T.get(("vstore", i - 2))], sig=True)
            T[("vstore", i)] = P.op("sp", "dma_start", out=Vd[i * 128:(i + 1) * 128, :], in_=vst[c][:, :],
                                    waits=[T[("vcopy", i)]], dsem=s_vst[c])
            P.op("act", "activation", out=junk2[:, :], in_=bank(6, 256), func=AF.Square, accum_out=st2[:, c:c + 1],
                 waits=[T[("U2", i)]])
            t_sq = P.op("act", "activation", out=st2[:, 2 + c:3 + c], in_=st2[:, c:c + 1], func=AF.Sqrt,
                        scale=1.0 / 256, bias=eps_t[:, :], dep=True, sig=True)
            T[("sq2", i)] = t_sq
            P.op("dve", "reciprocal", out=st2[:, 4 + c:5 + c], in_=st2[:, 2 + c:3 + c], waits=[t_sq])
            T[("ckvn", i)] = P.op("dve", "scalar_tensor_tensor", out=ckvn[c][:, :], in0=bank(6, 256),
                                  scalar=st2[:, 4 + c:5 + c], in1=kvg_bc[:, :], op0=ALU.mult, op1=ALU.mult,
                                  dep=True, waits=[T.get(("smalltr", i - 2))], sig=True)
            t_rd2 = rope_ops(bank(6, 64, 256).rearrange("p (s d) -> p s d", s=1), 1, cs[c], tC2, tS2, None,
                             [T[("U2", i)], t_sq, t_cs, T.get(("krr_add", i - 1))])
            T[("kr_rd", i)] = t_rd2
            P.op("pool", "tensor_tensor", out=krr[c][:, 0:64], in0=tC2[:, :], in1=tS2[:, :], op=ALU.add,
                 waits=[t_rd2, T.get(("smalltr", i - 2))])
            T[("krr_add", i)] = P.op("pool", "tensor_tensor", out=krr[c][:, 64:128], in0=tC2[:, :], in1=tS2[:, :],
                                     op=ALU.add, sig=True)

        def trK(i):
            c = i % 2
            wv = [T[("krot_add", i)], T.get(("KTev", i - 1))]
            for j in range(8):
                t = P.op("pe", "transpose", out=bankb(7, 128, j * 128), in_=krot[c][:, j * 128:(j + 1) * 128],
                         identity=ident[:, :], waits=wv, sig=(j == 7))
            T[("trK", i)] = t
            wv = [T[("ckvn", i)], T[("krr_add", i)], T.get(("smallev", i - 1)), T[("U2", i)], T[("kr_rd", i)],
                  T[("sq2", i)]]
            for j in range(2):
                P.op("pe", "transpose", out=bankb(6, 128, 640 + j * 128), in_=ckvn[c][:, j * 128:(j + 1) * 128],
                     identity=ident[:, :], waits=wv)
            T[("smalltr", i)] = P.op("pe", "transpose", out=bankb(6, 128, 896), in_=krr[c][:, :],
                                     identity=ident[:, :], waits=wv, sig=True)

        def evK(i):
            g = i // 2
            gc = g % 2
            T[("KTev", i)] = P.op("act", "activation", out=kst[gc][:, :, (i % 2) * 128:(i % 2) * 128 + 128],
                                  in_=bankb(7).rearrange("p (a b) -> p a b", a=8), func=AF.Copy,
                                  waits=[T[("trK", i)], T.get(("kstore", g - 2))], sig=True)
            if i % 2 == 1:
                T[("kstore", g)] = P.op("sp", "dma_start", out=KTv[:, :, g * 256:(g + 1) * 256], in_=kst[gc][:, :, :],
                                        waits=[T[("KTev", i)]], dsem=s_kst[gc])
            P.op("dve", "tensor_copy", out=ckvnT[:, :, i * 128:(i + 1) * 128],
                 in_=bankb(6, 256, 640).rearrange("p (a b) -> p a b", a=2), waits=[T[("smalltr", i)]])
            T[("smallev", i)] = P.op("dve", "tensor_copy", out=krT2[:, i * 128:(i + 1) * 128], in_=bankb(6, 128, 896),
                                     sig=True)

        import os as _os
        N = int(_os.environ.get("KN", NKT))
        NP.load(0)
        NP.load(1)
        NP.load(2)
        NP.stats(0)
        NP.stats(1)
        NP.transposes(0)
        NP.evac(0)
        for i in range(N):
            if i + 3 < N:
                NP.load(i + 3)
            if i + 2 < N:
                NP.stats(i + 2)
            if i + 1 < N:
                NP.transposes(i + 1)
                NP.evac(i + 1)
            proj_U1(i)
            if i >= 1:
                trK(i - 1)
                evK(i - 1)
            proj_U2(i)
            post(i)
        trK(N - 1)
        evK(N - 1)
        for s in s_kst + s_vst:
            P.op("sp", "wait_ge", s.h, s.n)
        if debug:
            t1 = P.op("sp", "dma_start", out=dbg_ckvnT, in_=ckvnT[:, :, :], waits=[P.last_tok("dve")], dsem=s_done)
            t2 = P.op("sp", "dma_start", out=dbg_krT, in_=krT2[:, :], dsem=s_done)
            P.op("sp", "wait_ge", s_done.h, s_done.n)
        P.flush()
        A.release(m)


    QTd = dscr("QTd", [128, 8, NQ], BF16)
    Wg_s = nc.dram_tensor("Wg_s", [16, 128, 6144], BF16, kind="Internal").ap()
    Wgu_s = nc.dram_tensor("Wgu_s", [FHC, 128, 4096], BF16, kind="Internal").ap()
    Wd_s = nc.dram_tensor("Wd_s", [16, 128, 5632], BF16, kind="Internal").ap()
    Wo_s = nc.dram_tensor("Wo_s", [8, 128, 4096], BF16, kind="Internal").ap()

    class Stager:
        def __init__(self, chunks, nbuf_elems):
            self.chunks = chunks
            self.bufs = [A.alloc("stg", [128, nbuf_elems], BF16) for _ in range(2)]
            self.s_in = [P.sem("stgi", sw=True) for _ in range(2)]
            self.s_out = [P.sem("stgo", sw=True) for _ in range(2)]
            self.t_out = {}
            self.t_in = {}
            self.c = 0
            if chunks:
                self.do_in(0)

        def do_in(self, c):
            parts, _, _ = self.chunks[c]
            sl = c % 2
            t = None
            for (lo, hi, vf, src) in parts:
                t = P.op("pool", "dma_start", out=vf(self.bufs[sl][:, lo:hi]), in_=src,
                         waits=[self.t_out.get(c - 2)], dsem=self.s_in[sl])
            self.t_in[c] = t

        def step(self, k):
            n = len(self.chunks)
            for _ in range(k):
                c = self.c
                if c >= n:
                    return
                if c + 1 < n:
                    self.do_in(c + 1)
                _, dst, nel = self.chunks[c]
                sl = c % 2
                self.t_out[c] = P.op("pool", "dma_start", out=dst, in_=self.bufs[sl][:, 0:nel],
                                     waits=[self.t_in[c]], dsem=self.s_out[sl])
                self.c += 1

        def finish(self):
            self.step(len(self.chunks))
            for s_ in self.s_out:
                P.op("pool", "wait_ge", s_.h, s_.n)

    w3in = w_in.rearrange("(k p) n -> p k n", p=128)
    woa3 = w_o_da.rearrange("(k p) n -> p k n", p=128)
    wob3 = w_o_mla.rearrange("(k p) n -> p k n", p=128)
    wg3 = w_gate.rearrange("(k p) n -> p k n", p=128)
    wu3 = w_up.rearrange("(k p) n -> p k n", p=128)
    wd3 = w_down.rearrange("(k p) n -> p k n", p=128)
    v16 = lambda ap: ap.rearrange("p (k c) -> p k c", k=16)
    v8 = lambda ap: ap.rearrange("p (k c) -> p k c", k=8)
    v22 = lambda ap: ap.rearrange("p (k c) -> p k c", k=22)
    chunks_g = []
    for fo in range(16):
        c0 = fo * 128
        chunks_g.append(([(0, 2048, v16, w3in[:, :, 3904 + c0:3904 + c0 + 128]),
                          (2048, 4096, v16, w3in[:, :, 5952 + c0:5952 + c0 + 128]),
                          (4096, 5120, v8, woa3[:, :, c0:c0 + 128]),
                          (5120, 6144, v8, wob3[:, :, c0:c0 + 128])], Wg_s[fo, :, :], 6144))
    chunks_gu = []
    for hc in range(FHC):
        c0 = hc * 128
        chunks_gu.append(([(0, 2048, v16, wg3[:, :, c0:c0 + 128]), (2048, 4096, v16, wu3[:, :, c0:c0 + 128])],
                          Wgu_s[hc, :, :], 4096))
    chunks_o = []
    wout3s = w_out.rearrange("(k p) n -> p k n", p=128)
    for cbh in range(8):
        chunks_o.append(([(0, 4096, v16, wout3s[:, :, cbh * 256:(cbh + 1) * 256])], Wo_s[cbh, :, :], 4096))
    chunks_d = []
    for half in range(2):
        for cb in range(8):
            chunks_d.append(([(0, 5632, v22, wd3[:, half * 22:(half + 1) * 22, cb * 256:(cb + 1) * 256])],
                             Wd_s[half * 8 + cb, :, :], 5632))

    def phase_Q():
        m = A.mark()
        wq = A.alloc("wq", [128, KC, 1536], BF16)
        s_w = P.sem("wq", sw=True)
        w3 = w_in.rearrange("(k p) n -> p k n", p=128)
        t_w = None
        for kq in range(4):
            P.op("pool", "dma_start", out=wq[:, kq * 4:(kq + 1) * 4, 0:1024], in_=w3[:, kq * 4:(kq + 1) * 4, 0:1024],
                 dsem=s_w)
            t_w = P.op("pool", "dma_start", out=wq[:, kq * 4:(kq + 1) * 4, 1024:1536],
                       in_=w3[:, kq * 4:(kq + 1) * 4, 3072:3584], dsem=s_w)
        NP = NormPipe("q", (0, 1), 2, lambda i: xq[i * 128:(i + 1) * 128, :], lambda i: (A1[:, :, 0], B1[:, :, 0]), NX=3)
        cs = [A.alloc("qcs", [128, 128], F32) for _ in range(2)]
        s_cs = [P.sem("qcs") for _ in range(2)]
        tC = A.alloc("qtC", [128, 1024], F32)
        tS = A.alloc("qtS", [128, 1024], F32)
        qrot = [A.alloc("qrot", [128, 1024], BF16) for _ in range(2)]
        cqn = [A.alloc("cqn", [128, 512], BF16) for _ in range(2)]
        st2 = A.alloc("qst2", [128, 8], F32)
        junk2 = A.alloc("qjunk2", [128, 512], BF16)
        qst = [A.alloc("qst", [128, 8, 256], BF16) for _ in range(2)]
        s_qst = [P.sem("qstd") for _ in range(2)]
        T = {}

        def proj(i):
            hT = NP.hT[i % 2]
            wv = [NP.t_hT[i], t_w, T.get(("rope_rd", i - 1)), T.get(("cqn", i - 1)), T.get(("sq2", i - 1))]
            for k in range(KC):
                for j in range(3):
                    t = P.op("pe", "matmul", out=bank(2 + j), lhsT=hT[:, k, :], rhs=wq[:, k, j * 512:(j + 1) * 512],
                             start=(k == 0), stop=(k == KC - 1), waits=wv, sig=(k == KC - 1 and j == 2))
            T[("U", i)] = t
            NP.hT_rel[i] = [t]

        def post(i):
            c = i % 2
            t_cs = P.op("sp", "dma_start", out=cs[c][:, :], in_=ropeQ[i * 128:(i + 1) * 128, :],
                        waits=[T.get(("rope_rd", i - 2))], dsem=s_cs[c])
            t_rd = rope_ops(bank(2, 1024).rearrange("p (s d) -> p s d", s=16), 16, cs[c], tC, tS, None,
                            [T[("U", i)], t_cs, T.get(("qrot_add", i - 1))])
            T[("rope_rd", i)] = t_rd
            T[("qrot_add", i)] = P.op("pool", "tensor_tensor", out=qrot[c][:, :], in0=tC[:, :], in1=tS[:, :],
                                      op=ALU.add, waits=[t_rd, T.get(("trQ", i - 2))], sig=True)
            P.op("act", "activation", out=junk2[:, :], in_=bank(4, 512), func=AF.Square, accum_out=st2[:, c:c + 1],
                 waits=[T[("U", i)]])
            t_sq = P.op("act", "activation", out=st2[:, 2 + c:3 + c], in_=st2[:, c:c + 1], func=AF.Sqrt,
                        scale=1.0 / 512, bias=eps_t[:, :], dep=True, sig=True)
            T[("sq2", i)] = t_sq
            P.op("dve", "reciprocal", out=st2[:, 4 + c:5 + c], in_=st2[:, 2 + c:3 + c], waits=[t_sq])
            T[("cqn", i)] = P.op("dve", "scalar_tensor_tensor", out=cqn[c][:, :], in0=bank(4, 512),
                                 scalar=st2[:, 4 + c:5 + c], in1=qg_bc[:, :], op0=ALU.mult, op1=ALU.mult,
                                 dep=True, waits=[T.get(("trQ", i - 2))], sig=True)

        def trQ(i):
            c = i % 2
            wv = [T[("qrot_add", i)], T.get(("Qev", i - 1))]
            for j in range(8):
                P.op("pe", "transpose", out=bankb(7, 128, j * 128), in_=qrot[c][:, j * 128:(j + 1) * 128],
                     identity=ident[:, :], waits=wv)
            wv = [T[("cqn", i)], T.get(("Qev", i - 1))]
            for j in range(4):
                t = P.op("pe", "transpose", out=bankb(6, 128, j * 128), in_=cqn[c][:, j * 128:(j + 1) * 128],
                         identity=ident[:, :], waits=wv, sig=(j == 3))
            T[("trQ", i)] = t

        def evQ(i):
            g = i // 2
            gc = g % 2
            P.op("act", "activation", out=qst[gc][:, :, (i % 2) * 128:(i % 2) * 128 + 128],
                 in_=bankb(7).rearrange("p (a b) -> p a b", a=8), func=AF.Copy,
                 waits=[T[("trQ", i)], T.get(("qstore", g - 2))])
            t1 = P.last_tok("act")
            if i % 2 == 1:
                T[("qstore", g)] = P.op("sp", "dma_start", out=QTd[:, :, g * 256:(g + 1) * 256], in_=qst[gc][:, :, :],
                                        waits=[t1], dsem=s_qst[gc])
            t2 = P.op("dve", "tensor_copy", out=cqnT[:, :, i * 128:(i + 1) * 128],
                      in_=bankb(6, 512).rearrange("p (a b) -> p a b", a=4), waits=[T[("trQ", i)]], sig=True)
            T[("Qev", i)] = [t1, t2]

        import os as _os
        N = int(_os.environ.get("QN", NQT))
        NP.load(0)
        NP.load(1)
        NP.load(2)
        NP.stats(0)
        NP.stats(1)
        NP.transposes(0)
        NP.evac(0)
        for i in range(N):
            if i + 3 < N:
                NP.load(i + 3)
            if i + 2 < N:
                NP.stats(i + 2)
            if i + 1 < N:
                NP.transposes(i + 1)
                NP.evac(i + 1)
            proj(i)
            if i >= 1:
                trQ(i - 1)
                evQ(i - 1)
            post(i)
        trQ(N - 1)
        evQ(N - 1)
        for s_ in s_qst:
            P.op("sp", "wait_ge", s_.h, s_.n)
        if debug:
            P.op("sp", "dma_start", out=dbg_cqnT, in_=cqnT[:, :, :], waits=[T[("Qev", N - 1)]], dsem=s_done)
            P.op("sp", "wait_ge", s_done.h, s_done.n)
        P.flush()
        A.release(m)


    def attention(name, nheads, head_setup, st_mms, scale, finish):
        pass

    def phase_MLA():
        m = A.mark()
        ropeQt = A.alloc("ropeQt", [128, NQT, 128], F32)
        wkvh = [A.alloc("wkvh", [128, 2, 256], BF16) for _ in range(2)]
        wuqh = [A.alloc("wuqh", [128, 4, 192], BF16) for _ in range(2)]
        knT = A.alloc("knT", [128, NTK], BF16)
        Vm = A.alloc("Vm", [128, NKT, 132], BF16)
        qnT = A.alloc("qnT", [128, NQ], BF16)
        qrT2 = A.alloc("qrT2", [128, NQ], BF16)
        qn_tok = [A.alloc("qn_tok", [128, 128], BF16) for _ in range(2)]
        qr_tok = [A.alloc("qr_tok", [128, 128], BF16) for _ in range(2)]
        tC2 = A.alloc("mtC2", [128, 64], F32)
        tS2 = A.alloc("mtS2", [128, 64], F32)
        NPT = 3
        PT = [A.alloc("mPT", [128, 1024], BF16) for _ in range(NPT)]
        oacc = A.alloc("moacc", [128, 4, 132], F32)
        rec = A.alloc("mrec", [128, 4], F32)
        o_tok = A.alloc("mo_tok", [128, 4, 128], BF16)
        s_w = [P.sem("mw", sw=True) for _ in range(2)]
        s_r = P.sem("mrope")
        t_rq = P.op("sp", "dma_start", out=ropeQt[:, :, :], in_=ropeQ.rearrange("(t p) c -> p t c", p=128), dsem=s_r)
        P.op("dve", "memset", Vm[:, :, 128:129], 1.0)
        t_ones = P.last_tok("dve")
        wkv3 = w_ukv.rearrange("(k p) n -> p k n", p=128)
        wuq3 = w_uq.rearrange("(k p) n -> p k n", p=128)
        T = {}
        stg = Stager(chunks_g + chunks_o + chunks_gu, 6144)
        pending = []
        STG = (0, 7)
        SP_ = ((1, 2), (3, 4))
        AB = (5, 6)
        cnt = {"st": 0}
        st_hist = {}
        NPP = NKT // 2
        for h in range(mla_heads):
            c = h % 2
            P.op("pool", "dma_start", out=wkvh[c][:, :, :], in_=wkv3[:, :, h * 256:(h + 1) * 256],
                 waits=[T.get(("wfree", h - 2))], dsem=s_w[c])
            t_wh = P.op("pool", "dma_start", out=wuqh[c][:, :, :], in_=wuq3[:, :, h * 192:(h + 1) * 192], dsem=s_w[c])
            stg.step(9)
            prev_att = [T.get(("att_done", h - 1)), T.get(("oev", h - 1, 3))]
            for blk in range(17):
                t0 = blk * 512
                n = 512 if blk < 16 else 256
                bk = STG[blk % 2]
                wv = [t_wh, T.get(("knev", h, blk - 2)), prev_att]
                for k in range(2):
                    t = P.op("pe", "matmul", out=bank(bk, n), lhsT=wkvh[c][:, k, 0:128], rhs=ckvnT[:, k, t0:t0 + n],
                             start=(k == 0), stop=(k == 1), waits=wv, sig=(k == 1))
                if blk % 2 == 0:
                    P.op("act", "activation", out=knT[:, t0:t0 + n], in_=bank(bk, n), func=AF.Copy, waits=[t])
                    T[("knev", h, blk)] = P.last_tok("act")
                else:
                    T[("knev", h, blk)] = P.op("dve", "tensor_copy", out=knT[:, t0:t0 + n], in_=bank(bk, n),
                                               waits=[t], sig=True)
            for g in range(17):
                bk = STG[(g + 1) % 2]
                nt = 4 if g < 16 else 2
                wv = [T.get(("vev", h, g - 2)), T.get(("knev", h, 16)), T.get(("knev", h, 15)), t_ones, prev_att]
                for tt in range(nt):
                    kt = g * 4 + tt
                    for k in range(2):
                        t = P.op("pe", "matmul", out=bank(bk, 128, tt * 128), lhsT=ckvnT[:, k, kt * 128:(kt + 1) * 128],
                                 rhs=wkvh[c][:, k, 128:256], start=(k == 0 and tt == 0), stop=(k == 1),
                                 skip_group_check=True, waits=wv, sig=(k == 1 and tt == nt - 1))
                src = bank(bk, nt * 128).rearrange("p (a b) -> p a b", a=nt)
                if g % 2 == 0:
                    T[("vev", h, g)] = P.op("dve", "tensor_copy", out=Vm[:, g * 4:g * 4 + nt, 0:128], in_=src,
                                            waits=[t], sig=True)
                else:
                    P.op("act", "activation", out=Vm[:, g * 4:g * 4 + nt, 0:128], in_=src, func=AF.Copy, waits=[t])
                    T[("vev", h, g)] = P.last_tok("act")
            for i in range(NQT):
                bk = STG[i % 2]
                wv = [T.get(("qev", h, i - 2)), T.get(("vev", h, 16)), T.get(("vev", h, 15)), prev_att]
                for k in range(4):
                    t = P.op("pe", "matmul", out=bank(bk, 192), lhsT=cqnT[:, k, i * 128:(i + 1) * 128],
                             rhs=wuqh[c][:, k, :], start=(k == 0), stop=(k == 3), waits=wv, sig=(k == 3))
                ci = i % 2
                P.op("act", "activation", out=qn_tok[ci][:, :], in_=bank(bk, 128), func=AF.Copy,
                     waits=[t, T.get(("qtr", h, i - 2))])
                t_qn = P.last_tok("act")
                t_rd = rope_ops(bank(bk, 64, 128).rearrange("p (s d) -> p s d", s=1), 1, ropeQt[:, i, :], tC2, tS2,
                                None, [t, t_qn, t_rq])
                P.op("dve", "tensor_tensor", out=qr_tok[ci][:, 0:64], in0=tC2[:, :], in1=tS2[:, :], op=ALU.add,
                     dep=True, waits=[T.get(("qtr", h, i - 2))])
                t_add = P.op("dve", "tensor_tensor", out=qr_tok[ci][:, 64:128], in0=tC2[:, :], in1=tS2[:, :],
                             op=ALU.add, sig=True)
                wv2 = [t_qn, t_add]
                P.op("pe", "transpose", out=bankb(bk, 128, 512), in_=qn_tok[ci][:, :], identity=ident[:, :], waits=wv2)
                t_tr = P.op("pe", "transpose", out=bankb(bk, 128, 640), in_=qr_tok[ci][:, :], identity=ident[:, :],
                            sig=True)
                T[("qtr", h, i)] = t_tr
                P.op("dve", "tensor_copy", out=qnT[:, i * 128:(i + 1) * 128], in_=bankb(bk, 128, 512), waits=[t_tr])
                t_e = P.op("dve", "tensor_copy", out=qrT2[:, i * 128:(i + 1) * 128], in_=bankb(bk, 128, 640), sig=True)
                T[("qev", h, i)] = [t_qn, t_add, t_e]
            t_qdone = T[("qev", h, NQT - 1)]
            for qb in (range(4) if mla_att else []):
                q0 = qb * 512
                wv_acc = [T.get(("accev", h, qb - 1)) if qb > 0 else T.get(("accev", h - 1, 3))]

                def emit_S(pp):
                    n = cnt["st"]
                    cnt["st"] += 1
                    p = SP_[n % 2]
                    wv = [t_qdone, st_hist.get(n - 2, {}).get("exp")]
                    for j in range(2):
                        kt = 2 * pp + j
                        P.op("pe", "matmul", out=bank(p[j]), lhsT=knT[:, kt * 128:(kt + 1) * 128],
                             rhs=qnT[:, q0:q0 + 512], start=True, stop=False, waits=wv)
                    for j in range(2):
                        kt = 2 * pp + j
                        t_s = P.op("pe", "matmul", out=bank(p[j]), lhsT=krT2[j * 64:(j + 1) * 64, kt * 128:(kt + 1) * 128],
                                   rhs=qrT2[j * 64:(j + 1) * 64, q0:q0 + 512], start=False, stop=True, sig=(j == 1))
                    st_hist[n] = {"s": t_s}
                    return n
                nq_ = [emit_S(0), emit_S(1)]
                for pp in range(NPP):
                    n = nq_.pop(0)
                    p = SP_[n % 2]
                    pt = PT[n % NPT]
                    t_e = P.op("act", "activation", out=pt[:, :], in_=ps[:, p[0] * 512:p[0] * 512 + 1024], func=AF.Exp,
                               scale=MLA_SCALE, waits=[st_hist[n]["s"], st_hist.get(n - NPT, {}).get("pv")], sig=True)
                    st_hist[n]["exp"] = t_e
                    if pp + 2 < NPP:
                        nq_.append(emit_S(pp + 2))
                    for j in range(2):
                        kt = 2 * pp + j
                        for qt in range(4):
                            t_pv = P.op("pe", "matmul", out=bank(AB[qt // 2], 129, (qt % 2) * 132),
                                        lhsT=pt[:, j * 512 + qt * 128: j * 512 + (qt + 1) * 128], rhs=Vm[:, kt, 0:129],
                                        start=(pp == 0 and j == 0 and qt % 2 == 0), stop=(pp == NPP - 1 and j == 1),
                                        skip_group_check=True, waits=[t_e] + (wv_acc if (pp == 0 and j == 0) else []),
                                        sig=(j == 1 and qt == 3))
                    st_hist[n]["pv"] = t_pv
                    st_hist.pop(n - 8, None)
                    if pp == 3 and pending:
                        pending.pop()()
                for a in range(2):
                    P.op("dve", "tensor_copy", out=oacc[:, a * 2:a * 2 + 2, :],
                         in_=bank(AB[a], 264).rearrange("p (a b) -> p a b", a=2), waits=[t_pv])
                t_ev = P.last_tok("dve")
                T[("accev", h, qb)] = t_ev
                P.op("dve", "reciprocal", out=rec[:, :], in_=oacc[:, :, 128], dep=True)
                P.op("dve", "tensor_tensor", out=o_tok[:, :, :], in0=oacc[:, :, 0:128], in1=bc_last(rec[:, :], 128),
                     op=ALU.mult, dep=True, waits=[T.get(("otr", h, qb - 1)) if qb > 0 else T.get(("otr", h - 1, 3))])
                t_o = P.last_tok("dve")

                def fin(h=h, qb=qb, q0=q0, t_o=t_o):
                    wv = [t_o, T.get(("oev", h, qb - 1)) if qb > 0 else T.get(("oev", h - 1, 3)),
                          T.get(("qev", h, NQT - 2))]
                    for qt in range(4):
                        t = P.op("pe", "transpose", out=bankb(0, 128, qt * 128), in_=o_tok[:, qt, :],
                                 identity=ident[:, :], waits=wv, sig=(qt == 3))
                    T[("otr", h, qb)] = t
                    P.op("act", "activation", out=oT_mla[:, h, q0:q0 + 512], in_=bankb(0, 512), func=AF.Copy,
                         waits=[t])
                    T[("oev", h, qb)] = P.last_tok("act")
                if qb < 3:
                    pending.append(fin)
                else:
                    fin()
            T[("att_done", h)] = P.last_tok("pe")
            T[("wfree", h)] = T[("att_done", h)]
        stg.finish()
        if debug and mla_att:
            P.op("sp", "dma_start", out=dbg_oTm, in_=oT_mla[:, :, :], waits=[T[("oev", mla_heads - 1, 3)]], dsem=s_done)
            P.op("sp", "wait_ge", s_done.h, s_done.n)
        P.flush()
        A.release(m)

    def phase_DA():
        m = A.mark()
        QdaT = A.alloc("QdaT", [128, 8, NQ], BF16)
        s_qd = P.sem("qdld")
        t_qd = None
        for hq in range(4):
            t_qd = P.op("sp", "dma_start", out=QdaT[:, hq * 2:hq * 2 + 2, :], in_=QTd[:, hq * 2:hq * 2 + 2, :], dsem=s_qd)
        KTh = [A.alloc("KTh", [128, NTK], BF16) for _ in range(2)]
        Vh = [A.alloc("Vh", [128, NKT, 132], BF16) for _ in range(2)]
        NPT = 3
        PT = [A.alloc("dPT", [128, 1024], BF16) for _ in range(NPT)]
        oacc = A.alloc("doacc", [128, 8, 132], F32)
        rec = A.alloc("drec", [128, 8], F32)
        t0_ = A.alloc("dt0", [128, 4, 128], F32)
        t1_ = A.alloc("dt1", [128, 4, 128], F32)
        osq = A.alloc("dosq", [128, 4, 128], F32)
        ssq = A.alloc("dssq", [128, 8], F32)
        o_tok = A.alloc("do_tok", [128, 4, 128], BF16)
        subs = A.alloc("dsubs", [128, 128], F32)
        s_k = [P.sem("dk") for _ in range(2)]
        for c in range(2):
            P.op("dve", "memset", Vh[c][:, :, 128:129], 1.0)
        P.op("dve", "tensor_scalar", out=subs[:, :], in0=subln[:, :], scalar1=1.0 - LAM_INIT, scalar2=None,
             op0=ALU.mult)
        t_ones = P.last_tok("dve")
        Vd3 = Vd.rearrange("(t p) c -> p t c", p=128)
        T = {}
        stg = Stager(chunks_d, 5632)
        pending = []
        SP_ = ((1, 2), (3, 4))
        AB = (5, 6, 7)
        cnt = {"st": 0}
        st_hist = {}
        t_pv = None
        for h in range(8):
            c = h % 2
            P.op("sp", "dma_start", out=KTh[c][:, :], in_=KTd[h, :, :], waits=[T.get(("att_done", h - 2)), t_ones],
                 dsem=s_k[c])
            for part in range(3):
                t_kv = P.op("sp", "dma_start", out=Vh[c][:, part * 22:(part + 1) * 22, 0:128],
                            in_=Vd3[:, part * 22:(part + 1) * 22, h * 128:(h + 1) * 128], dsem=s_k[c])
            for qb in range(4):
                q0 = qb * 512
                prev = T.get("last_accev")

                def emit_S(kt):
                    n = cnt["st"]
                    cnt["st"] += 1
                    p = SP_[n % 2]
                    for u in range(2):
                        t_s = P.op("pe", "matmul", out=bank(p[u]),
                                   lhsT=KTh[c][u * 64:(u + 1) * 64, kt * 128:(kt + 1) * 128],
                                   rhs=QdaT[u * 64:(u + 1) * 64, h, q0:q0 + 512], start=True, stop=True,
                                   waits=[t_kv, t_qd, st_hist.get(n - 2, {}).get("exp")], sig=(u == 1))
                    st_hist[n] = {"s": t_s}
                    return n
                nq_ = [emit_S(0), emit_S(1)]
                for kt in range(NKT):
                    n = nq_.pop(0)
                    p = SP_[n % 2]
                    pt = PT[n % NPT]
                    t_e = P.op("act", "activation", out=pt[:, :], in_=ps[:, p[0] * 512:p[0] * 512 + 1024], func=AF.Exp,
                               scale=DA_SCALE, waits=[st_hist[n]["s"], st_hist.get(n - NPT, {}).get("pv")], sig=True)
                    st_hist[n]["exp"] = t_e
                    if kt + 2 < NKT:
                        nq_.append(emit_S(kt + 2))
                    for u in range(2):
                        for qt in range(4):
                            g = u * 4 + qt
                            t_pv = P.op("pe", "matmul", out=bank(AB[g // 3], 129, (g % 3) * 132),
                                        lhsT=pt[:, u * 512 + qt * 128: u * 512 + (qt + 1) * 128], rhs=Vh[c][:, kt, 0:129],
                                        start=(kt == 0 and g % 3 == 0), stop=(kt == NKT - 1), skip_group_check=True,
                                        waits=[t_e] + ([prev] if kt == 0 else []), sig=(g == 7))
                    st_hist[n]["pv"] = t_pv
                    st_hist.pop(n - 8, None)
                    if kt == 6 and pending:
                        pending.pop()()
                P.op("dve", "tensor_copy", out=oacc[:, 0:3, :], in_=bank(AB[0], 396).rearrange("p (a b) -> p a b", a=3),
                     waits=[t_pv])
                P.op("dve", "tensor_copy", out=oacc[:, 3:6, :], in_=bank(AB[1], 396).rearrange("p (a b) -> p a b", a=3))
                P.op("dve", "tensor_copy", out=oacc[:, 6:8, :], in_=bank(AB[2], 264).rearrange("p (a b) -> p a b", a=2))
                T["last_accev"] = P.last_tok("dve")
                P.op("dve", "reciprocal", out=rec[:, :], in_=oacc[:, :, 128], dep=True)
                P.op("dve", "tensor_scalar", out=rec[:, 4:8], in0=rec[:, 4:8], scalar1=lamt[:, 0:1], scalar2=None,
                     op0=ALU.mult, dep=True)
                P.op("dve", "tensor_tensor", out=t0_[:, :, :], in0=oacc[:, 0:4, 0:128], in1=bc_last(rec[:, 0:4], 128),
                     op=ALU.mult, dep=True)
                P.op("dve", "tensor_tensor", out=t1_[:, :, :], in0=oacc[:, 4:8, 0:128], in1=bc_last(rec[:, 4:8], 128),
                     op=ALU.mult)
                P.op("dve", "tensor_tensor", out=t0_[:, :, :], in0=t0_[:, :, :], in1=t1_[:, :, :], op=ALU.add, dep=True)
                P.op("dve", "tensor_tensor", out=osq[:, :, :], in0=t0_[:, :, :], in1=t0_[:, :, :], op=ALU.mult, dep=True)
                P.op("dve", "tensor_reduce", out=ssq[:, 0:4], in_=osq[:, :, :], axis=AXX, op=ALU.add, dep=True)
                t_ss = P.last_tok("dve")
                t_sq = P.op("act", "activation", out=ssq[:, 4:8], in_=ssq[:, 0:4], func=AF.Sqrt, scale=1.0 / 128,
                            bias=eps_t[:, :], waits=[t_ss], sig=True)
                P.op("dve", "reciprocal", out=ssq[:, 4:8], in_=ssq[:, 4:8], waits=[t_sq])
                P.op("dve", "tensor_tensor", out=osq[:, :, :], in0=t0_[:, :, :], in1=bc_last(ssq[:, 4:8], 128),
                     op=ALU.mult, dep=True)
                P.op("dve", "tensor_tensor", out=o_tok[:, :, :], in0=osq[:, :, :], in1=bc_mid(subs[:, :], 4),
                     op=ALU.mult, dep=True, waits=[T.get("last_otr")])
                t_o = P.last_tok("dve")

                def fin(h=h, q0=q0, t_o=t_o):
                    wv = [t_o, T.get("last_oev")]
                    for qt in range(4):
                        t = P.op("pe", "transpose", out=bankb(0, 128, qt * 128), in_=o_tok[:, qt, :],
                                 identity=ident[:, :], waits=wv, sig=(qt == 3))
                    T["last_otr"] = t
                    P.op("act", "activation", out=oT_da[:, h, q0:q0 + 512], in_=bankb(0, 512), func=AF.Copy, waits=[t])
                    T["last_oev"] = P.last_tok("act")
                pending.append(fin)
            T[("att_done", h)] = t_pv
        while pending:
            pending.pop()()
        stg.finish()
        if debug:
            P.op("sp", "dma_start", out=dbg_oTd, in_=oT_da[:, :, :], waits=[T["last_oev"]], dsem=s_done)
            P.op("sp", "wait_ge", s_done.h, s_done.n)
        P.flush()
        A.release(m)

    wout3 = w_out.rearrange("(k p) n -> p k n", p=128)
    PAIRS = ((2, 3), (4, 5), (6, 7))

    def phase_OF(j):
        t0 = j * 512
        mj = A.mark()
        xres = [A.alloc("xres", [128, D], F32) for _ in range(4)]
        hTb = A.alloc("hTb", [128, KC, 512], BF16)
        mO = A.mark()
        g1_bc = A.alloc("g1bc", [128, D], F32)
        mT = A.alloc("mT", [128, KC, 512], BF16)

        def ab(i):
            return (A1[:, :, 0], B1[:, :, 0]) if i < 4 else (A2[:, :], B2)
        NP = NormPipe("o", (0, 1), 4, lambda i: xq[t0 + i * 128: t0 + (i + 1) * 128, :], ab, NX=4, xt=xres,
                      out_fn=lambda i: hTb[:, :, (i % 4) * 128:(i % 4 + 1) * 128])
        mL = A.mark()
        wg = [A.alloc("wg", [128, 6144], BF16) for _ in range(2)]
        sg = [A.alloc("sg", [128, 512], F32) for _ in range(2)]
        mA_ = [A.alloc("mA", [128, 512], F32) for _ in range(2)]
        tB = A.alloc("tB", [128, 512], F32)
        s_g1 = P.sem("g1")
        s_wg = [P.sem("wg") for _ in range(2)]
        t_g1 = P.op("sp", "dma_start", out=g1_bc[:, :], in_=g_d[:, 0:D], dsem=s_g1)
        for i in range(4):
            NP.load(i)
        NP.stats(0)
        NP.stats(1)
        for i in range(4):
            NP.transposes(i)
            NP.evac(i)
            if i + 2 < 4:
                NP.stats(i + 2)
        t_h1 = [NP.t_hT[i] for i in range(4)]
        bfree = {}
        wg_free = {}
        T = {}
        n_unit = 0
        for fo in range(16):
            c = fo % 2
            t_w = P.op("sp", "dma_start", out=wg[c][:, :], in_=Wg_s[fo, :, :], waits=[wg_free.get(fo - 2)],
                       dsem=s_wg[c])
            for br in range(2):
                pa = PAIRS[n_unit % 3]
                n_unit += 1
                goff = br * 2048
                yoff = 4096 + br * 1024
                oT = oT_da if br == 0 else oT_mla
                wv = [t_h1, t_w, bfree.get(pa[0]), bfree.get(pa[1])]
                for k in range(KC):
                    P.op("pe", "matmul", out=bank(pa[0]), lhsT=wg[c][:, goff + k * 128: goff + (k + 1) * 128],
                         rhs=hTb[:, k, :], start=(k == 0), stop=(k == KC - 1), waits=wv)
                for k in range(8):
                    t = P.op("pe", "matmul", out=bank(pa[1]), lhsT=wg[c][:, yoff + k * 128: yoff + (k + 1) * 128],
                             rhs=oT[:, k, t0:t0 + 512], start=(k == 0), stop=(k == 7), sig=(k == 7))
                if br == 1:
                    wg_free[fo] = t
                u = n_unit % 2
                t_s = P.op("act", "activation", out=sg[u][:, :], in_=bank(pa[0]), func=AF.Sigmoid,
                           waits=[t, T.get(("sgfree", n_unit - 2))], sig=True)
                bfree[pa[0]] = t_s
                if br == 0:
                    a = fo % 2
                    t_m = P.op("dve", "tensor_tensor", out=mA_[a][:, :], in0=bank(pa[1]), in1=sg[u][:, :], op=ALU.mult,
                               waits=[t_s, T.get(("mAfree", fo - 2))], sig=True)
                    T[("sgfree", n_unit)] = t_m
                    bfree[pa[1]] = t_m
                else:
                    t_m = P.op("dve", "tensor_tensor", out=tB[:, :], in0=bank(pa[1]), in1=sg[u][:, :], op=ALU.mult,
                               waits=[t_s, T.get(("tBfree", fo - 1))], sig=True)
                    T[("sgfree", n_unit)] = t_m
                    bfree[pa[1]] = t_m
                    t_p = P.op("pool", "tensor_tensor", out=mT[:, fo, :], in0=mA_[fo % 2][:, :], in1=tB[:, :], op=ALU.add,
                               waits=[t_m], sig=True)
                    T[("mAfree", fo)] = t_p
                    T[("tBfree", fo)] = t_p
        t_mT = T[("mAfree", 15)]
        P.flush()
        A.release(mL)
        wo = [A.alloc("wo", [128, KC, 512], BF16) for _ in range(2)]
        tr_ = [A.alloc("tres", [128, 512], F32) for _ in range(2)]
        s_wo = [P.sem("wo") for _ in range(2)]
        wo_free = {}
        bfree = {}
        T = {}
        n_unit = 0
        RB = (2, 3, 4, 5, 6, 7)
        t_x1 = {}
        for cb in range(4):
            c = cb % 2
            for hf in range(2):
                t_w = P.op("sp", "dma_start", out=wo[c][:, :, hf * 256:(hf + 1) * 256],
                           in_=Wo_s[cb * 2 + hf, :, :].rearrange("p (k c) -> p k c", k=16),
                           waits=[wo_free.get(cb - 2)], dsem=s_wo[c])
            for i in range(4):
                b = RB[n_unit % 6]
                wv = [t_w, bfree.get(b)]
                for k in range(KC):
                    t = P.op("pe", "matmul", out=bank(b), lhsT=mT[:, k, i * 128:(i + 1) * 128], rhs=wo[c][:, k, :],
                             start=(k == 0), stop=(k == KC - 1), waits=wv, sig=(k == KC - 1))
                u = n_unit % 2
                t_d = P.op("dve", "tensor_tensor", out=tr_[u][:, :], in0=bank(b), in1=g1_bc[:, cb * 512:(cb + 1) * 512],
                           op=ALU.mult, waits=[t, t_g1, T.get(("trfree", n_unit - 2))], sig=True)
                bfree[b] = t_d
                t_p = P.op("pool", "tensor_tensor", out=xres[i][:, cb * 512:(cb + 1) * 512],
                           in0=xres[i][:, cb * 512:(cb + 1) * 512], in1=tr_[u][:, :], op=ALU.add,
                           waits=[t_d, NP.t_xn[i]], sig=True)
                T[("trfree", n_unit)] = t_p
                t_x1[i] = t_p
                n_unit += 1
            wo_free[cb] = t
        t_lastmm = t
        for i in range(2):
            NP.stats(4 + i, src=xres[i][:, :], src_wait=t_x1[i])
        for i in range(4):
            ii = 4 + i
            NP.transposes(ii)
            NP.hT_rel[ii - 4] = [t_lastmm]
            NP.evac(ii)
            if i + 2 < 4:
                NP.stats(ii + 2, src=xres[i + 2][:, :], src_wait=t_x1[i + 2])
        P.flush()
        A.release(mO)
        g2_bc = A.alloc("g2bc", [128, D], F32)
        fin_bc = A.alloc("finbc", [128, D], F32)
        actT = A.alloc("actT", [128, 22, 512], BF16)
        wgu = [A.alloc("wgu", [128, 4096], BF16) for _ in range(2)]
        wd = [A.alloc("wd", [128, 5632], BF16) for _ in range(2)]
        sl_ = [A.alloc("silu", [128, 512], F32) for _ in range(2)]
        td_ = [A.alloc("tdn", [128, 256], F32) for _ in range(2)]
        st3 = A.alloc("fst", [128, 8], F32)
        junk3 = A.alloc("fjunk", [128, D], BF16)
        s_g2 = P.sem("g2")
        s_wgu = [P.sem("wgu") for _ in range(2)]
        s_wd = [P.sem("wd") for _ in range(2)]
        s_out = [P.sem("outst") for _ in range(2)]
        P.op("sp", "dma_start", out=g2_bc[:, :], in_=g_d[:, D:2 * D], dsem=s_g2)
        t_g2 = P.op("sp", "dma_start", out=fin_bc[:, :], in_=fin_bc_d, dsem=s_g2)
        bfree = {}
        T = {}
        n_pair = 0
        n_dn = 0
        wgu_free = {}
        wd_free = {}
        n_wgu = 0
        n_wd = 0
        t_acc = {}
        for half in range(2):
            for hc in range(22):
                hcg = half * 22 + hc
                c = n_wgu % 2
                t_w = P.op("sp", "dma_start", out=wgu[c][:, :], in_=Wgu_s[hcg, :, :], waits=[wgu_free.get(n_wgu - 2)],
                           dsem=s_wgu[c])
                pa = PAIRS[n_pair % 3]
                wv = [t_w, bfree.get(pa[0]), bfree.get(pa[1]), T.get("actT_free") if hc == 0 else None]
                for k in range(KC):
                    P.op("pe", "matmul", out=bank(pa[0]), lhsT=wgu[c][:, k * 128:(k + 1) * 128], rhs=hTb[:, k, :],
                         start=(k == 0), stop=(k == KC - 1), waits=wv)
                for k in range(KC):
                    t = P.op("pe", "matmul", out=bank(pa[1]), lhsT=wgu[c][:, 2048 + k * 128:2048 + (k + 1) * 128],
                             rhs=hTb[:, k, :], start=(k == 0), stop=(k == KC - 1), sig=(k == KC - 1))
                wgu_free[n_wgu] = t
                n_wgu += 1
                u = n_pair % 2
                t_s = P.op("act", "activation", out=sl_[u][:, :], in_=bank(pa[0]), func=AF.Silu,
                           waits=[t, T.get(("slfree", n_pair - 2))], sig=True)
                bfree[pa[0]] = t_s
                t_m = P.op("dve", "tensor_tensor", out=actT[:, hc, :], in0=bank(pa[1]), in1=sl_[u][:, :], op=ALU.mult,
                           waits=[t_s], sig=True)
                bfree[pa[1]] = t_m
                T[("slfree", n_pair)] = t_m
                n_pair += 1
            t_act = t_m
            for cb in range(8):
                c = n_wd % 2
                t_w = P.op("sp", "dma_start", out=wd[c][:, :], in_=Wd_s[half * 8 + cb, :, :], waits=[wd_free.get(n_wd - 2)],
                           dsem=s_wd[c])
                for i in range(4):
                    b = RB[n_dn % 6]
                    wv = [t_w, t_act, bfree.get(b)]
                    for k in range(22):
                        t = P.op("pe", "matmul", out=bank(b, 256), lhsT=actT[:, k, i * 128:(i + 1) * 128],
                                 rhs=wd[c][:, k * 256:(k + 1) * 256], start=(k == 0), stop=(k == 21), waits=wv,
                                 sig=(k == 21))
                    u = n_dn % 2
                    t_d = P.op("dve", "tensor_tensor", out=td_[u][:, :], in0=bank(b, 256),
                               in1=g2_bc[:, cb * 256:(cb + 1) * 256], op=ALU.mult,
                               waits=[t, t_g2, T.get(("tdfree", n_dn - 2))], sig=True)
                    bfree[b] = t_d
                    t_p = P.op("pool", "tensor_tensor", out=xres[i][:, cb * 256:(cb + 1) * 256],
                               in0=xres[i][:, cb * 256:(cb + 1) * 256], in1=td_[u][:, :], op=ALU.add, waits=[t_d],
                               sig=True)
                    T[("tdfree", n_dn)] = t_p
                    t_acc[i] = t_p
                    n_dn += 1
                wd_free[n_wd] = t
                n_wd += 1
            T["actT_free"] = t
        for i in range(4):
            c = i % 2
            P.op("act", "activation", out=junk3[:, :], in_=xres[i][:, :], func=AF.Square, accum_out=st3[:, c:c + 1],
                 waits=[t_acc[i]])
            t_sq = P.op("act", "activation", out=st3[:, 2 + c:3 + c], in_=st3[:, c:c + 1], func=AF.Sqrt,
                        scale=1.0 / D, bias=eps_t[:, :], dep=True, sig=True)
            P.op("dve", "reciprocal", out=st3[:, 4 + c:5 + c], in_=st3[:, 2 + c:3 + c], waits=[t_sq, t_acc[i]])
            t_f = P.op("dve", "scalar_tensor_tensor", out=xres[i][:, :], in0=xres[i][:, :], scalar=st3[:, 4 + c:5 + c],
                       in1=fin_bc[:, :], op0=ALU.mult, op1=ALU.mult, dep=True, sig=True)
            P.op("sp", "dma_start", out=out_d[t0 + i * 128:t0 + (i + 1) * 128, :], in_=xres[i][:, :], waits=[t_f],
                 dsem=s_out[c])
        for s_ in s_out:
            P.op("sp", "wait_ge", s_.h, s_.n)
        P.flush()
        A.release(mj)

    if "A" not in skip:
        phase_A()
    if stop_after == "A":
        return nc, dbg
    if "K" not in skip:
        phase_K()
    if stop_after == "K":
        return nc, dbg
    cqnT = A.alloc("cqnT", [128, 4, NQ], BF16, top=True)
    if "Q" not in skip:
        phase_Q()
    if stop_after == "Q":
        return nc, dbg
    oT_mla = A.alloc("oT_mla", [128, 8, NQ], BF16)
    phase_MLA()
    A.hi = SB_END
    if stop_after == "MLA":
        return nc, dbg
    oT_da = A.alloc("oT_da", [128, 8, NQ], BF16)
    phase_DA()
    if stop_after == "DA":
        return nc, dbg
    for j in range(4):
        phase_OF(j)
    return nc, dbg


def _rope_tables(pos_row, pos_col):
    nf = 16
    inv = (10000.0 ** (-np.arange(nf, dtype=np.float32) / nf)).astype(np.float32)
    ar = pos_row[:, None].astype(np.float32) * inv
    ac = pos_col[:, None].astype(np.float32) * inv
    cr, sr, cc, sc = np.cos(ar), np.sin(ar), np.cos(ac), np.sin(ac)
    C64 = np.concatenate([cr, cr, cc, cc], 1)
    S64 = np.concatenate([-sr, sr, -sc, sc], 1)
    return np.concatenate([C64, S64], 1).astype(np.float32)


def make_in_maps(inputs):
    f = lambda a: np.ascontiguousarray(np.asarray(a, dtype=np.float32))
    x = f(inputs["x"]); c = f(inputs["c"]); ctx = f(inputs["ctx"]); c_ctx = f(inputs["c_ctx"])
    t = np.arange(8192)
    ropeL = _rope_tables(t // 64, t % 64)
    ropeC = np.concatenate([np.ones((256, 64), np.float32), np.zeros((256, 64), np.float32)], 1)
    ropeK = np.concatenate([ropeC, ropeL], 0)
    b_ada = f(inputs["b_ada"])[0]
    shared = {
        "w_ada": f(inputs["w_ada"])[0],
        "b_adaT": np.ascontiguousarray(b_ada.reshape(96, 128).T),
        "b_ada_g": np.ascontiguousarray(np.broadcast_to(
            np.concatenate([b_ada[2 * D:3 * D], b_ada[5 * D:6 * D]])[None, :], (128, 2 * D))),
        "n1gT": np.ascontiguousarray(f(inputs["norm1_g"])[0].reshape(KC, 128).T),
        "n2gT": np.ascontiguousarray(f(inputs["norm2_g"])[0].reshape(KC, 128).T),
        "fin_bc": np.ascontiguousarray(np.broadcast_to(f(inputs["final_norm_g"])[None, :], (128, D))),
        "w_in": f(inputs["w_in"])[0],
        "lam_in": np.ascontiguousarray(np.broadcast_to(f(inputs["da_lambda"])[0].reshape(1, 256), (128, 256))),
        "subln_bc": np.ascontiguousarray(np.broadcast_to(f(inputs["da_subln_g"])[0][None, :], (128, 128))),
        "qg_bc": np.ascontiguousarray(np.broadcast_to(f(inputs["mla_q_norm_g"])[0][None, :], (128, 512))),
        "kvg_bc": np.ascontiguousarray(np.broadcast_to(f(inputs["mla_kv_norm_g"])[0][None, :], (128, 256))),
        "w_uq": f(inputs["w_uq"])[0], "w_ukv": f(inputs["w_ukv"])[0],
        "w_o_da": f(inputs["w_o_da"])[0], "w_o_mla": f(inputs["w_o_mla"])[0], "w_out": f(inputs["w_out"])[0],
        "w_gate": f(inputs["w_ffn_gate"])[0], "w_up": f(inputs["w_ffn_up"])[0], "w_down": f(inputs["w_ffn_down"])[0],
        "ropeK": ropeK, "ident": np.eye(128, dtype=np.float32),
    }
    maps = []
    for core in range(8):
        b, qs = core // 4, core % 4
        m = dict(shared)
        m["xk"] = np.ascontiguousarray(np.concatenate([ctx[b], x[b]], 0))
        m["xq"] = np.ascontiguousarray(x[b, qs * NQ:(qs + 1) * NQ])
        cT = np.stack([c[b].reshape(KC, 128).T, c_ctx.reshape(KC, 128).T], -1)
        m["cT"] = np.ascontiguousarray(cT)
        m["ropeQ"] = np.ascontiguousarray(ropeL[qs * NQ:(qs + 1) * NQ])
        maps.append(m)
    return maps


def kernel(**inputs):
    nc, _ = build_nc()
    maps = make_in_maps(inputs)
    res = run_bass_kernel_spmd(nc, maps, core_ids=list(range(8)))
    out = np.zeros((2, 8192, D), np.float32)
    for core in range(8):
        b, qs = core // 4, core % 4
        out[b, qs * NQ:(qs + 1) * NQ] = res.results[core]["out"]
    return out
```
